# Optimizing a Trainium2 kernel written in Bass

```python
import jax, jax.numpy as jnp
from jax import lax
import numpy as np

D_MODEL = 4096
BATCH = 2
SEQ = 8192
DEPTH = 1
DEC_BATCH = 2
DEC_SEQ = 4096
PAST_LEN = 128

HEAD_DIM = 128
N_HEADS_A = 16
N_HEADS_B = 16
WIDTH_A = N_HEADS_A * HEAD_DIM
WIDTH_B = N_HEADS_B * HEAD_DIM
MIX_WIDTH = WIDTH_A + WIDTH_B
D_FF = 11008
DILATED_CONFIGS = ((128, 1), (512, 4), (2048, 16))
BLK = 128
ROPE_THETA = 500000.0
ROT_DIM = HEAD_DIM // 4
GRID_W = 64
WIN_ROWS = 8
WIN_COLS = 16
NORM_EPS = 1e-6
NEG_INF = -1e30

kernel_name = "hybrid_dilated_neighbourhood_encoder"


def _rmsnorm(x, g):
    xf = x.astype(jnp.float32)
    y = xf * lax.rsqrt(jnp.mean(xf * xf, axis=-1, keepdims=True) + NORM_EPS)
    return (y * g.astype(jnp.float32)).astype(x.dtype)


def _swiglu(x, w_gate, w_up, w_down):
    return (jax.nn.silu(x @ w_gate) * (x @ w_up)) @ w_down


def _rope_partial(x):
    T = x.shape[1]
    pos = jnp.arange(T, dtype=jnp.float32)
    inv = ROPE_THETA ** (-jnp.arange(0, ROT_DIM, 2, dtype=jnp.float32) / ROT_DIM)
    ang = pos[:, None] * inv[None, :]
    cos = jnp.cos(ang)[None, :, None, :]
    sin = jnp.sin(ang)[None, :, None, :]
    xf = x.astype(jnp.float32)
    x1 = xf[..., :ROT_DIM // 2]
    x2 = xf[..., ROT_DIM // 2:ROT_DIM]
    out = jnp.concatenate([x1 * cos - x2 * sin, x2 * cos + x1 * sin, xf[..., ROT_DIM:]], axis=-1)
    return out.astype(x.dtype)


def _window_partials(q, k, v, half):
    B, G, L, H, dh = q.shape
    Lp = ((L + BLK - 1) // BLK) * BLK
    nb = Lp // BLK
    qb = jnp.pad(q, ((0, 0), (0, 0), (0, Lp - L), (0, 0), (0, 0))).reshape(B, G, nb, BLK, H, dh)
    pad = ((0, 0), (0, 0), (BLK, Lp - L + BLK), (0, 0), (0, 0))
    kp = jnp.pad(k, pad).reshape(B, G, nb + 2, BLK, H, dh)
    vp = jnp.pad(v, pad).reshape(B, G, nb + 2, BLK, H, dh)
    kn = jnp.concatenate([kp[:, :, :-2], kp[:, :, 1:-1], kp[:, :, 2:]], axis=3)
    vn = jnp.concatenate([vp[:, :, :-2], vp[:, :, 1:-1], vp[:, :, 2:]], axis=3)
    blk = jnp.arange(nb)[:, None] * BLK
    qpos = blk + jnp.arange(BLK)[None, :]
    kpos = blk - BLK + jnp.arange(3 * BLK)[None, :]
    valid_k = (kpos >= 0) & (kpos < L)
    mask = (jnp.abs(qpos[:, :, None] - kpos[:, None, :]) <= half) & valid_k[:, None, :]
    s = jnp.einsum('bgnqhd,bgnkhd->bgnhqk', qb, kn).astype(jnp.float32) * (HEAD_DIM ** -0.5)
    s = jnp.where(mask[None, None, :, None], s, NEG_INF)
    m = jnp.max(s, axis=-1)
    p = jnp.exp(s - m[..., None])
    den = jnp.sum(p, axis=-1)
    o = jnp.einsum('bgnhqk,bgnkhd->bgnqhd', p.astype(v.dtype), vn).astype(jnp.float32)
    o = o / jnp.swapaxes(den, -1, -2)[..., None]
    o = o.reshape(B, G, Lp, H, dh)[:, :, :L]
    m = jnp.swapaxes(m, -1, -2).reshape(B, G, Lp, H)[:, :, :L]
    den = jnp.swapaxes(den, -1, -2).reshape(B, G, Lp, H)[:, :, :L]
    return o, m, den


def _dilated_attention(q, k, v):
    B, T, H, dh = q.shape
    outs, ms, dens = [], [], []
    for window, dil in DILATED_CONFIGS:
        half = window // (2 * dil)
        L = T // dil

        def strided(x):
            return x.reshape(B, L, dil, H, dh).transpose(0, 2, 1, 3, 4)

        o, m, den = _window_partials(strided(q), strided(k), strided(v), half)
        outs.append(o.transpose(0, 2, 1, 3, 4).reshape(B, T, H, dh))
        ms.append(m.transpose(0, 2, 1, 3).reshape(B, T, H))
        dens.append(den.transpose(0, 2, 1, 3).reshape(B, T, H))
    m_all = jnp.stack(ms)
    den_all = jnp.stack(dens)
    o_all = jnp.stack(outs)
    w = den_all * jnp.exp(m_all - jnp.max(m_all, axis=0, keepdims=True))
    o = jnp.sum(w[..., None] * o_all, axis=0) / jnp.sum(w, axis=0)[..., None]
    return o.astype(q.dtype)


def _neighbourhood_attention(q, k, v, rel_bias):
    B, T, H, dh = q.shape
    rows = T // GRID_W
    wr = min(WIN_ROWS, rows)
    r = jnp.arange(rows)
    rs = jnp.clip(r - wr // 2, 0, rows - wr)
    row_idx = rs[:, None] + jnp.arange(wr)[None, :]
    c = jnp.arange(GRID_W)
    cs = jnp.clip(c - WIN_COLS // 2, 0, GRID_W - WIN_COLS)
    col_mask = (c[None, :] >= cs[:, None]) & (c[None, :] < cs[:, None] + WIN_COLS)
    qg = q.reshape(B, rows, GRID_W, H, dh)
    kr = k.reshape(B, rows, GRID_W, H, dh)[:, row_idx]
    vr = v.reshape(B, rows, GRID_W, H, dh)[:, row_idx]
    s = jnp.einsum('brqhd,brjkhd->brhqjk', qg, kr).astype(jnp.float32) * (HEAD_DIM ** -0.5)
    dr = row_idx - r[:, None] + (WIN_ROWS - 1)
    dc = jnp.clip(c[None, :] - c[:, None], -(WIN_COLS - 1), WIN_COLS - 1) + (WIN_COLS - 1)
    b = rel_bias.astype(jnp.float32)[:, dr]
    b = b[:, :, :, dc].transpose(1, 0, 3, 2, 4)
    s = s + b[None]
    s = jnp.where(col_mask[None, None, None, :, None, :], s, NEG_INF)
    p = jax.nn.softmax(s.reshape(B, rows, H, GRID_W, wr * GRID_W), axis=-1)
    p = p.reshape(B, rows, H, GRID_W, wr, GRID_W).astype(v.dtype)
    o = jnp.einsum('brhqjk,brjkhd->brqhd', p, vr)
    return o.reshape(B, T, H, dh)


def _layer(h, ffn1_norm, ffn1_w_gate, ffn1_w_up, ffn1_w_down, mix_norm, w_in, nbr_rel_bias,
           out_norm_a, out_norm_b, w_out, ffn2_norm, ffn2_w_gate, ffn2_w_up, ffn2_w_down):
    B, T, _ = h.shape
    h = h + 0.5 * _swiglu(_rmsnorm(h, ffn1_norm), ffn1_w_gate, ffn1_w_up, ffn1_w_down)
    u = _rmsnorm(h, mix_norm)
    qkv = u @ w_in
    splits = [WIDTH_A, 2 * WIDTH_A, 3 * WIDTH_A, 3 * WIDTH_A + WIDTH_B, 3 * WIDTH_A + 2 * WIDTH_B]
    qa, ka, va, qb, kb, vb = jnp.split(qkv, splits, axis=-1)
    heads_a = lambda t: t.reshape(B, T, N_HEADS_A, HEAD_DIM)
    heads_b = lambda t: t.reshape(B, T, N_HEADS_B, HEAD_DIM)
    qa, ka, va = _rope_partial(heads_a(qa)), _rope_partial(heads_a(ka)), heads_a(va)
    oa = _dilated_attention(qa, ka, va).reshape(B, T, WIDTH_A)
    ob = _neighbourhood_attention(heads_b(qb), heads_b(kb), heads_b(vb), nbr_rel_bias).reshape(B, T, WIDTH_B)
    merged = jnp.concatenate([_rmsnorm(oa, out_norm_a), _rmsnorm(ob, out_norm_b)], axis=-1)
    h = h + merged @ w_out
    h = h + 0.5 * _swiglu(_rmsnorm(h, ffn2_norm), ffn2_w_gate, ffn2_w_up, ffn2_w_down)
    return h


def setup_inputs(seed: int = 0) -> dict:
    key = jax.random.key(seed)
    ks = jax.random.split(key, 20)
    f32 = jnp.float32

    def normal(k, shape, scale):
        return jax.random.normal(k, shape, f32) * scale

    def gain(k, width):
        return 1.0 + 0.01 * jax.random.normal(k, (DEPTH, width), f32)

    return {
        "x_prompt": jax.random.normal(ks[0], (BATCH, SEQ, D_MODEL), f32),
        "x_sample": jax.random.normal(ks[1], (DEC_BATCH, DEC_SEQ, D_MODEL), f32),
        "ffn1_norm": gain(ks[2], D_MODEL),
        "ffn1_w_gate": normal(ks[3], (DEPTH, D_MODEL, D_FF), D_MODEL ** -0.5),
        "ffn1_w_up": normal(ks[4], (DEPTH, D_MODEL, D_FF), D_MODEL ** -0.5),
        "ffn1_w_down": normal(ks[5], (DEPTH, D_FF, D_MODEL), D_FF ** -0.5),
        "mix_norm": gain(ks[6], D_MODEL),
        "w_in": normal(ks[7], (DEPTH, D_MODEL, 3 * MIX_WIDTH), D_MODEL ** -0.5),
        "nbr_rel_bias": normal(ks[8], (DEPTH, N_HEADS_B, 2 * WIN_ROWS - 1, 2 * WIN_COLS - 1), 0.1),
        "out_norm_a": gain(ks[9], WIDTH_A),
        "out_norm_b": gain(ks[10], WIDTH_B),
        "w_out": normal(ks[11], (DEPTH, MIX_WIDTH, D_MODEL), MIX_WIDTH ** -0.5),
        "ffn2_norm": gain(ks[12], D_MODEL),
        "ffn2_w_gate": normal(ks[13], (DEPTH, D_MODEL, D_FF), D_MODEL ** -0.5),
        "ffn2_w_up": normal(ks[14], (DEPTH, D_MODEL, D_FF), D_MODEL ** -0.5),
        "ffn2_w_down": normal(ks[15], (DEPTH, D_FF, D_MODEL), D_FF ** -0.5),
        "final_norm": 1.0 + 0.01 * jax.random.normal(ks[16], (D_MODEL,), f32),
    }


def reference(x_prompt, x_sample, ffn1_norm, ffn1_w_gate, ffn1_w_up, ffn1_w_down, mix_norm, w_in,
              nbr_rel_bias, out_norm_a, out_norm_b, w_out, ffn2_norm, ffn2_w_gate, ffn2_w_up,
              ffn2_w_down, final_norm):
    def run(x):
        h = x
        for layer in range(DEPTH):
            h = _layer(h, ffn1_norm[layer], ffn1_w_gate[layer], ffn1_w_up[layer], ffn1_w_down[layer],
                       mix_norm[layer], w_in[layer], nbr_rel_bias[layer], out_norm_a[layer],
                       out_norm_b[layer], w_out[layer], ffn2_norm[layer], ffn2_w_gate[layer],
                       ffn2_w_up[layer], ffn2_w_down[layer])
        return _rmsnorm(h, final_norm)

    y_prompt = run(x_prompt)
    y_sample = run(x_sample)
    return (y_prompt, y_sample)
```

```python
import numpy as np
import concourse.bass as bass
import concourse.mybir as mybir
from concourse.bass_utils import run_bass_kernel_spmd

F32 = mybir.dt.float32
BF16 = mybir.dt.bfloat16
AF = mybir.ActivationFunctionType
ALU = mybir.AluOpType
AX = mybir.AxisListType

HALO = 1024
TT = 512
ROPE_THETA = 500000.0
EPS = 1e-6


class Cfg:
    def __init__(self, D=4096, DFF=11008, HA=16, HB=16, NCORES=8, OWN=3072,
                 SEQS=(8192, 8192, 4096, 4096)):
        self.D, self.DFF, self.HA, self.HB = D, DFF, HA, HB
        self.NCORES, self.OWN, self.SEQS = NCORES, OWN, tuple(SEQS)
        self.KC = D // 128
        self.FC = DFF // 128
        self.FG = self.FC // 2
        self.NH = HA + HB
        self.EXT = OWN + 2 * HALO
        self.NTE = self.EXT // TT
        self.NTO = OWN // TT
        self.HT = HALO // TT
        assert self.FC % 2 == 0 and HA % 2 == 0 and HB % 2 == 0
        assert (HA + HB) * 128 == D and self.KC % 4 == 0
        assert sum(SEQS) == NCORES * OWN
        self.WSLOT = max(2 * self.KC * 128, self.KC * 256, self.FG * 128)
        self.ABYTES = max(self.FG * TT * 2, 43008, 2 * D * 4)


class Op:
    __slots__ = ("eng", "emit", "deps", "chan", "idx")


class Sched:
    def __init__(self):
        self.ops = []
        self.lastw = {}
        self.readers = {}
        self.eng_count = {}
        self.chan_count = {}

    def add(self, eng, emit, reads=(), writes=(), chan=None):
        op = Op()
        op.eng, op.emit, op.chan = eng, emit, chan
        deps = set()
        for r in reads:
            w = self.lastw.get(r)
            if w is not None:
                deps.add(w)
        for w_ in writes:
            w = self.lastw.get(w_)
            if w is not None:
                deps.add(w)
            rs = self.readers.get(w_)
            if rs:
                deps.update(rs)
        op.deps = deps
        for r in reads:
            self.readers.setdefault(r, []).append(op)
        for w_ in writes:
            self.lastw[w_] = op
            self.readers[w_] = []
        if chan is None:
            self.eng_count[eng] = self.eng_count.get(eng, 0) + 1
            op.idx = self.eng_count[eng]
        else:
            self.chan_count[chan] = self.chan_count.get(chan, 0) + 16
            op.idx = self.chan_count[chan]
        self.ops.append(op)
        return op


def build_program(cfg):
    D, KC, FC, FG = cfg.D, cfg.KC, cfg.FC, cfg.FG
    HA, HB, NH = cfg.HA, cfg.HB, cfg.NH
    OWN, EXT, NTE, NTO, HT = cfg.OWN, cfg.EXT, cfg.NTE, cfg.NTO, cfg.HT
    NBLK = EXT // 128
    WSLOT = cfg.WSLOT
    NBW = 2
    SCALE = 128.0 ** -0.5

    nc = bass.Bass("TRN2", target_bir_lowering=False)

    def din(name, shape, dt=F32):
        return nc.dram_tensor(name, list(shape), dt, kind="ExternalInput").ap()

    x_ext = din("x_ext", [EXT, D])
    wgu = [din("wgu1", [FC, 128, 2 * KC * 128]), din("wgu2", [FC, 128, 2 * KC * 128])]
    wd = [din("wd1", [2 * KC, 128, FG * 128]), din("wd2", [2 * KC, 128, FG * 128])]
    wqk = din("wqk", [NH, 128, 2 * KC * 128])
    wv = din("wv", [NH // 2, 128, KC * 256])
    wo = din("wo", [KC // 2, 128, 2 * KC * 128])
    gains_d = din("gains", [128, 5 * KC])
    ropeC_d = din("ropeC", [128, EXT])
    ropeS_d = din("ropeS", [128, EXT])
    consts_d = din("consts", [128, 256])
    relb_d = din("relb", [HB, 465])
    gb_d = din("gb", [HB, 128, 8 * TT])
    maskA_d = din("maskA", [128, 20 * TT])
    vflag_d = din("vflag", [128, NTO * 20])
    maskB_d = din("maskB", [NTO, 128, 8 * TT])
    y_d = nc.dram_tensor("y", [OWN, D], F32, kind="ExternalOutput").ap()
    hscr = nc.dram_tensor("hscr", [NTO, 128, KC * TT], F32, kind="Internal").ap()
    qscr = nc.dram_tensor("qscr", [NH, 128, OWN], BF16, kind="Internal").ap()
    kscr = nc.dram_tensor("kscr", [NH, 128, EXT], BF16, kind="Internal").ap()
    vscr = nc.dram_tensor("vscr", [NH, 128, NBLK * 128], BF16, kind="Internal").ap()

    S = Sched()
    import contextlib
    es = contextlib.ExitStack()
    with es:
        def sb(name, shape, dt):
            return es.enter_context(nc.sbuf_tensor("s_" + name, list(shape), dt))

        Hs = sb("H", [128, KC, TT], F32)
        Xs = sb("X", [128, KC, TT], BF16)
        Ar = sb("A", [128, cfg.ABYTES // 2], BF16)
        Wr = [sb(f"W{i}", [128, WSLOT], BF16) for i in range(NBW)]
        ident = sb("ident", [128, 128], F32)
        piT = sb("piT", [128, 128], F32)
        identb = sb("identb", [128, 128], BF16)
        onesb = sb("onesb", [128, 128], BF16)
        gains = sb("gains", [128, 5 * KC], F32)
        RS = sb("RS", [128, TT], F32)
        RS2 = sb("RS2", [128, TT], F32)
        SQ = [sb(f"SQ{i}", [128, TT], BF16) for i in range(2)]
        TS = [sb(f"TS{i}", [128, TT], BF16) for i in range(2)]
        CT = sb("CT", [128, TT], F32)
        ST = sb("ST", [128, TT], F32)
        QF = [sb(f"QF{i}", [128, TT], F32) for i in range(2)]
        T1 = sb("T1", [128, TT], F32)
        T2 = sb("T2", [128, TT], F32)
        OUTB = [sb(f"OUTB{i}", [128, TT], BF16) for i in range(2)]
        VO = [sb(f"VO{i}", [128, 256], BF16) for i in range(2)]
        KMAX = sb("KMAX", [128, NH], F32)
        QMAX = sb("QMAX", [128, NH], F32)
        NEGM = sb("NEGM", [128, NH], F32)
        BSS = sb("BSS", [128, HB], F32)
        TM = [sb(f"TM{i}", [128, 1], F32) for i in range(2)]
        VFL = sb("VFL", [128, NTO * 20], F32)
        RD = sb("RD", [128, TT], F32)
        PS = [es.enter_context(nc.psum_tensor(f"ps{i}", [128, TT], F32)) for i in range(8)]

        A_act = Ar[:, 0:FG * TT].rearrange("p (f t) -> p f t", t=TT)
        XS = [Ar[:, i * 2 * D:(i + 1) * 2 * D].bitcast(F32) for i in range(2)]
        o = 0
        QT = []; KT = []; VB = []; GB = []; EB = []; PB = []
        for i in range(2):
            QT.append(Ar[:, o:o + TT]); o += TT
        for i in range(2):
            KT.append(Ar[:, o:o + 2560]); o += 2560
        for i in range(2):
            VB.append(Ar[:, o:o + 2560].rearrange("p (b d) -> p b d", d=128)); o += 2560
        for i in range(2):
            GB.append(Ar[:, o:o + 8 * TT].rearrange("p (b t) -> p b t", t=TT)); o += 8 * TT
        for i in range(2):
            EB.append(Ar[:, o:o + TT]); o += TT
        for i in range(2):
            PB.append(Ar[:, o:o + TT]); o += TT
        assert o * 2 <= cfg.ABYTES
        A_RES = 1024

        def ares(lo_el, n_el):
            lo = lo_el * 2; hi = (lo_el + n_el) * 2
            return [("A", i) for i in range(lo // A_RES, (hi + A_RES - 1) // A_RES)]

        r_act = lambda f: ares(f * TT, TT)
        r_xs = lambda i: ares(i * 2 * D, 2 * D)
        _o = [0]

        def _nx(n):
            r = ares(_o[0], n); _o[0] += n; return r
        r_QT = [_nx(TT) for i in range(2)]
        r_KT = [_nx(2560) for i in range(2)]
        r_VB = [_nx(2560) for i in range(2)]
        r_GB = [_nx(8 * TT) for i in range(2)]
        r_EB = [_nx(TT) for i in range(2)]
        r_PB = [_nx(TT) for i in range(2)]
        assert 20 * TT <= 2 * WSLOT or True
        if WSLOT >= 8192:
            MA = lambda kb: (Wr[0][:, kb * TT:(kb + 1) * TT] if kb < 16 else Wr[1][:, (kb - 16) * TT:(kb - 15) * TT])
            MBv = lambda kb: Wr[1][:, (4 + kb) * TT:(5 + kb) * TT]
            ma_dst = [(Wr[0][:, 0:16 * TT], 0, 16), (Wr[1][:, 0:4 * TT], 16, 4)]
            mb_dst = Wr[1][:, 4 * TT:12 * TT]
            r_mask = [("W", 0), ("W", 1)]
            extra_masks = None
        else:
            MAt = sb("MAt", [128, 20 * TT], BF16)
            MBt = sb("MBt", [128, 8 * TT], BF16)
            MA = lambda kb: MAt[:, kb * TT:(kb + 1) * TT]
            MBv = lambda kb: MBt[:, kb * TT:(kb + 1) * TT]
            ma_dst = [(MAt[:, :], 0, 20)]
            mb_dst = MBt[:, :]
            r_mask = [("MASK",)]

        bank_rr = [0]

        def gbank():
            b = bank_rr[0] % 6
            bank_rr[0] += 1
            return b

        wslot_rr = [0]

        def wload(src_ap, nel):
            s = wslot_rr[0] % NBW
            wslot_rr[0] += 1
            S.add("pool", lambda e, s=s, src_ap=src_ap, nel=nel: e.dma_start(out=Wr[s][:, 0:nel], in_=src_ap),
                  writes=[("W", s)], chan=("W", s))
            return s

        pe_defer = []

        def flush_defer():
            while pe_defer:
                f = pe_defer.pop(0)
                f()

        def mm_group(bank, nmm, lhs_fn, rhs_fn, reads, out_ap=None):
            outp = PS[bank][:, :] if out_ap is None else out_ap

            def emit(e):
                ins = None
                for k in range(nmm):
                    ins = e.matmul(outp, lhsT=lhs_fn(k), rhs=rhs_fn(k), start=(k == 0), stop=(k == nmm - 1))
                return ins
            S.add("pe", emit, reads=reads, writes=[("PS", bank)])

        S.add("sp", lambda e: e.dma_start(out=ident[:, :], in_=consts_d[:, 0:128]), writes=["ident"], chan="c_id")
        S.add("sp", lambda e: e.dma_start(out=piT[:, :], in_=consts_d[:, 128:256]), writes=["piT"], chan="c_pi")
        S.add("sp", lambda e: e.dma_start(out=gains[:, :], in_=gains_d[:, :]), writes=["gains"], chan="c_g")
        S.add("sp", lambda e: e.dma_start(out=VFL[:, :], in_=vflag_d[:, :]), writes=["VFL"], chan="c_vf")
        S.add("dve", lambda e: e.memset(onesb[:, :], 1.0), writes=["onesb"])
        S.add("dve", lambda e: e.tensor_copy(out=identb[:, :], in_=ident[:, :]), reads=["ident"], writes=["identb"])
        S.add("dve", lambda e: e.memset(KMAX[:, :], 0.0), writes=["KMAX"])
        S.add("dve", lambda e: e.memset(QMAX[:, :], 0.0), writes=["QMAX"])
        S.add("dve", lambda e: e.memset(BSS[:, :], 0.0), writes=["BSS"])

        r_H = lambda k: ("H", k)
        r_X = lambda k: ("X", k)

        def norm(gidx, final=False):
            for k in range(KC):
                S.add("act", lambda e, k=k: e.activation(out=SQ[k % 2][:, :], in_=Hs[:, k, :], func=AF.Square),
                      reads=[r_H(k)], writes=[("SQ", k % 2)])
                S.add("pe", lambda e, k=k: e.matmul(PS[6][:, :], lhsT=onesb[:, :], rhs=SQ[k % 2][:, :],
                                                   start=(k == 0), stop=(k == KC - 1)),
                      reads=[("SQ", k % 2), "onesb"], writes=[("PS", 6)])
            S.add("dve", lambda e: e.tensor_scalar(out=RS[:, :], in0=PS[6][:, :], scalar1=1.0 / D, scalar2=EPS,
                                                   op0=ALU.mult, op1=ALU.add),
                  reads=[("PS", 6)], writes=["RS"])
            S.add("act", lambda e: e.activation(out=RS[:, :], in_=RS[:, :], func=AF.Sqrt), reads=["RS"], writes=["RS"])
            S.add("dve", lambda e: e.reciprocal(out=RS[:, :], in_=RS[:, :]), reads=["RS"], writes=["RS"])
            for k in range(KC):
                if final:
                    S.add("dve", lambda e, k=k: e.scalar_tensor_tensor(
                        out=Hs[:, k, :], in0=Hs[:, k, :], scalar=gains[:, gidx * KC + k:gidx * KC + k + 1],
                        in1=RS[:, :], op0=ALU.mult, op1=ALU.mult),
                        reads=[r_H(k), "RS", "gains"], writes=[r_H(k)])
                else:
                    S.add("dve", lambda e, k=k: e.scalar_tensor_tensor(
                        out=Xs[:, k, :], in0=Hs[:, k, :], scalar=gains[:, gidx * KC + k:gidx * KC + k + 1],
                        in1=RS[:, :], op0=ALU.mult, op1=ALU.mult),
                        reads=[r_H(k), "RS", "gains"], writes=[r_X(k)])

        def ffn(li):
            xr = [r_X(k) for k in range(KC)]
            for g in range(2):
                for fl in range(FG):
                    f = g * FG + fl
                    s = wload(wgu[li][f], 2 * KC * 128)
                    wv_ = Wr[s][:, 0:2 * KC * 128].rearrange("p (j k c) -> p j k c", j=2, k=KC)
                    bg = gbank(); bu = gbank()
                    mm_group(bg, KC, lambda k, wv_=wv_: wv_[:, 0, k, :], lambda k: Xs[:, k, :], reads=[("W", s)] + xr)
                    mm_group(bu, KC, lambda k, wv_=wv_: wv_[:, 1, k, :], lambda k: Xs[:, k, :], reads=[("W", s)] + xr)
                    flush_defer()
                    S.add("act", lambda e, bg=bg, fl=fl: e.activation(out=TS[fl % 2][:, :], in_=PS[bg][:, :], func=AF.Silu),
                          reads=[("PS", bg)], writes=[("TS", fl % 2)])
                    S.add("dve", lambda e, bu=bu, fl=fl: e.tensor_tensor(out=A_act[:, fl, :], in0=TS[fl % 2][:, :],
                                                                       in1=PS[bu][:, :], op=ALU.mult),
                          reads=[("PS", bu), ("TS", fl % 2)], writes=r_act(fl))
                ar = [r for fl in range(FG) for r in r_act(fl)]
                for m in range(KC):
                    s = wload(wd[li][g * KC + m], FG * 128)
                    wv_ = Wr[s][:, 0:FG * 128].rearrange("p (k c) -> p k c", k=FG)
                    b = gbank()
                    mm_group(b, FG, lambda k, wv_=wv_: wv_[:, k, :], lambda k: A_act[:, k, :], reads=[("W", s)] + ar)
                    S.add("dve", lambda e, b=b, m=m: e.scalar_tensor_tensor(
                        out=Hs[:, m, :], in0=PS[b][:, :], scalar=0.5, in1=Hs[:, m, :], op0=ALU.mult, op1=ALU.add),
                        reads=[("PS", b), r_H(m)], writes=[r_H(m)])

        def qk_chunk(t, idx, h, is_q, is_a, wview, s, j):
            xr = [r_X(k) for k in range(KC)]
            b = gbank()
            mm_group(b, KC, lambda k: wview[:, j, k, :], lambda k: Xs[:, k, :], reads=[("W", s)] + xr)
            flush_defer()
            ob = idx % 2
            sc = SCALE if is_q else 1.0
            if is_a:
                S.add("act", lambda e: e.activation(out=QF[ob][:, :], in_=PS[b][:, :], func=AF.Copy, scale=sc),
                      reads=[("PS", b)], writes=[("QF", ob)])

                def d1():
                    S.add("pe", lambda e: e.matmul(PS[7][:, :], lhsT=piT[:, :], rhs=QF[ob][:, :], start=True, stop=True),
                          reads=[("QF", ob), "piT"], writes=[("PS", 7)])
                    S.add("dve", lambda e: e.tensor_tensor(out=T1[:, :], in0=QF[ob][:, :], in1=CT[:, :], op=ALU.mult),
                          reads=[("QF", ob), "CT"], writes=["T1"])
                    S.add("dve", lambda e: e.tensor_tensor(out=T2[:, :], in0=PS[7][:, :], in1=ST[:, :], op=ALU.mult),
                          reads=[("PS", 7), "ST"], writes=["T2"])
                    S.add("dve", lambda e: e.tensor_tensor(out=OUTB[ob][:, :], in0=T1[:, :], in1=T2[:, :], op=ALU.add),
                          reads=["T1", "T2"], writes=[("OUTB", ob)])
                    tail()
                pe_defer.append(d1)
            else:
                S.add("act", lambda e: e.activation(out=OUTB[ob][:, :], in_=PS[b][:, :], func=AF.Copy, scale=sc),
                      reads=[("PS", b)], writes=[("OUTB", ob)])
                pe_defer.append(lambda: tail())

            def tail():
                S.add("act", lambda e: e.activation(out=SQ[ob][:, :], in_=OUTB[ob][:, :], func=AF.Square),
                      reads=[("OUTB", ob)], writes=[("SQ", ob)])
                S.add("pe", lambda e: e.matmul(PS[6][:, :], lhsT=onesb[:, :], rhs=SQ[ob][:, :], start=True, stop=True),
                      reads=[("SQ", ob), "onesb"], writes=[("PS", 6)])
                S.add("dve", lambda e: e.tensor_reduce(out=TM[ob][:, :], in_=PS[6][:, :], axis=AX.X, op=ALU.max),
                      reads=[("PS", 6)], writes=[("TM", ob)])
                MX = QMAX if is_q else KMAX
                mxn = "QMAX" if is_q else "KMAX"
                S.add("dve", lambda e: e.tensor_tensor(out=MX[:, h:h + 1], in0=MX[:, h:h + 1], in1=TM[ob][:, :], op=ALU.max),
                      reads=[("TM", ob), mxn], writes=[mxn])
                if is_q:
                    to = t - HT
                    dst = qscr[h, :, to * TT:(to + 1) * TT]
                else:
                    dst = kscr[h, :, t * TT:(t + 1) * TT]
                S.add("sp", lambda e: e.dma_start(out=dst, in_=OUTB[ob][:, :]), reads=[("OUTB", ob)],
                      writes=[("scr", "q" if is_q else "k", h, t)], chan=("OUTB", ob))

        for t in range(NTE):
            own = HT <= t < HT + NTO
            S.add("sp", lambda e, t=t: e.dma_start(out=CT[:, :], in_=ropeC_d[:, t * TT:(t + 1) * TT]), writes=["CT"], chan="ropeC")
            S.add("sp", lambda e, t=t: e.dma_start(out=ST[:, :], in_=ropeS_d[:, t * TT:(t + 1) * TT]), writes=["ST"], chan="ropeS")
            for tb in range(4):
                sl = tb % 2
                r0 = t * TT + tb * 128
                S.add("sp", lambda e, sl=sl, r0=r0: e.dma_start(out=XS[sl], in_=x_ext[r0:r0 + 128, :]),
                      writes=r_xs(sl), chan=("XS", sl))
                for kq in range(KC // 4):
                    b = gbank()

                    def emit(e, sl=sl, kq=kq, b=b):
                        ins = None
                        for j in range(4):
                            ins = e.transpose(PS[b][:, j * 128:(j + 1) * 128], XS[sl][:, (kq * 4 + j) * 128:(kq * 4 + j + 1) * 128], ident[:, :])
                        return ins
                    S.add("pe", emit, reads=r_xs(sl) + ["ident"], writes=[("PS", b)])
                    eng = "dve" if kq % 2 == 0 else "act"
                    outv = Hs[:, kq * 4:(kq + 1) * 4, tb * 128:(tb + 1) * 128]
                    inv = PS[b][:, :].rearrange("p (a c) -> p a c", a=4)
                    if eng == "dve":
                        S.add("dve", lambda e, outv=outv, inv=inv: e.tensor_copy(out=outv, in_=inv),
                              reads=[("PS", b)], writes=[r_H(kq * 4 + j) for j in range(4)])
                    else:
                        S.add("act", lambda e, outv=outv, inv=inv: e.activation(out=outv, in_=inv, func=AF.Copy),
                              reads=[("PS", b)], writes=[r_H(kq * 4 + j) for j in range(4)])
            norm(0)
            ffn(0)
            norm(1)
            if own:
                S.add("sp", lambda e, t=t: e.dma_start(out=hscr[t - HT], in_=Hs[:, :, :].rearrange("p k t -> p (k t)")),
                      reads=[r_H(k) for k in range(KC)], writes=[("hscr", t - HT)], chan="hst")
            idx = 0
            for pr in range(NH // 2):
                s = wload(wqk[pr], 2 * KC * 128)
                wview = Wr[s][:, 0:2 * KC * 128].rearrange("p (j k c) -> p j k c", j=2, k=KC)
                for j in range(2):
                    h = pr * 2 + j
                    qk_chunk(t, idx, h, False, h < HA, wview, s, j); idx += 1
            xr = [r_X(k) for k in range(KC)]
            vcnt = 0
            for pn in range(NH // 2):
                s = wload(wv[pn], KC * 256)
                wview = Wr[s][:, 0:KC * 256].rearrange("p (k c) -> p k c", k=KC)
                for tb in range(4):
                    b = gbank()
                    mm_group(b, KC, lambda k, tb=tb: Xs[:, k, tb * 128:(tb + 1) * 128], lambda k, wview=wview: wview[:, k, :],
                             reads=[("W", s)] + xr, out_ap=PS[b][:, 0:256])
                    flush_defer()
                    vb = vcnt % 2; vcnt += 1
                    S.add("act", lambda e, b=b, vb=vb: e.activation(out=VO[vb][:, :], in_=PS[b][:, 0:256], func=AF.Copy),
                          reads=[("PS", b)], writes=[("VO", vb)])
                    blk = t * 4 + tb
                    dst = vscr[2 * pn:2 * pn + 2, :, blk * 128:(blk + 1) * 128].rearrange("h p d -> p h d")
                    S.add("sp", lambda e, dst=dst, vb=vb: e.dma_start(out=dst, in_=VO[vb][:, :].rearrange("p (h d) -> p h d", h=2)),
                          reads=[("VO", vb)], writes=[("scr", "v", pn, blk)], chan=("VO", vb))
            if own:
                for pr in range(NH // 2):
                    s = wload(wqk[NH // 2 + pr], 2 * KC * 128)
                    wview = Wr[s][:, 0:2 * KC * 128].rearrange("p (j k c) -> p j k c", j=2, k=KC)
                    for j in range(2):
                        h = pr * 2 + j
                        qk_chunk(t, idx, h, True, h < HA, wview, s, j); idx += 1
            flush_defer()

        for hb in range(HB):
            S.add("sp", lambda e, hb=hb: e.dma_start(out=T1[:, 0:465], in_=relb_d[hb:hb + 1, :].to_broadcast([128, 465])),
                  writes=["T1"], chan="relb")
            S.add("act", lambda e, hb=hb: e.activation(out=T2[:, 0:465], in_=T1[:, 0:465], func=AF.Square,
                                                       accum_out=BSS[:, hb:hb + 1]),
                  reads=["T1", "BSS"], writes=["T2", "BSS"])
        S.add("dve", lambda e: e.tensor_tensor(out=NEGM[:, :], in0=QMAX[:, :], in1=KMAX[:, :], op=ALU.mult),
              reads=["QMAX", "KMAX"], writes=["NEGM"])
        S.add("act", lambda e: e.activation(out=NEGM[:, :], in_=NEGM[:, :], func=AF.Sqrt), reads=["NEGM"], writes=["NEGM"])
        S.add("act", lambda e: e.activation(out=BSS[:, :], in_=BSS[:, :], func=AF.Sqrt), reads=["BSS"], writes=["BSS"])
        S.add("dve", lambda e: e.tensor_scalar(out=NEGM[:, :], in0=NEGM[:, :], scalar1=-1.02, scalar2=None, op0=ALU.mult),
              reads=["NEGM"], writes=["NEGM"])
        S.add("dve", lambda e: e.tensor_tensor(out=NEGM[:, HA:NH], in0=NEGM[:, HA:NH], in1=BSS[:, :], op=ALU.subtract),
              reads=["NEGM", "BSS"], writes=["NEGM"])

        def attn_loads(ti, h):
            par = h % 2
            i0 = (ti + HT) * TT
            is_a = h < HA
            klo = i0 - 1024 if is_a else i0 - 256
            nk = 2560 if is_a else 1024
            S.add("sp", lambda e: e.dma_start(out=QT[par], in_=qscr[h, :, ti * TT:(ti + 1) * TT]),
                  reads=[("scr", "q", h, ti + HT)], writes=r_QT[par], chan=("QT", par))
            S.add("sp", lambda e: e.dma_start(out=KT[par][:, 0:nk], in_=kscr[h, :, klo:klo + nk]),
                  reads=[("scr", "k", h, tt) for tt in range(klo // TT, (klo + nk - 1) // TT + 1)], writes=r_KT[par], chan=("KT", par))
            b0 = klo // 128
            nb = nk // 128
            S.add("sp", lambda e: e.dma_start(out=VB[par][:, 0:nb, :].rearrange("p b d -> p (b d)"), in_=vscr[h, :, b0 * 128:(b0 + nb) * 128]),
                  reads=[("scr", "v", h // 2, bb) for bb in range(b0, b0 + nb)], writes=r_VB[par], chan=("VB", par))
            if not is_a:
                S.add("pool", lambda e: e.dma_start(out=GB[par].rearrange("p b t -> p (b t)"), in_=gb_d[h - HA]),
                      writes=r_GB[par], chan=("GB", par))

        for ti in range(NTO):
            for (dst, k0, n) in ma_dst:
                S.add("pool", lambda e, dst=dst, k0=k0, n=n: e.dma_start(out=dst, in_=maskA_d[:, k0 * TT:(k0 + n) * TT]),
                      writes=r_mask, chan="MA")
            S.add("pool", lambda e, ti=ti: e.dma_start(out=mb_dst, in_=maskB_d[ti]), writes=r_mask, chan="MA")
            attn_loads(ti, 0)
            for h in range(NH):
                if h + 1 < NH:
                    attn_loads(ti, h + 1)
                par = h % 2
                is_a = h < HA
                nkb = 20 if is_a else 8
                bO = 2 + par; bD = 4 + par
                bSS = 6 if is_a else 7
                rl = r_QT[par] + r_KT[par]

                def s_op(kb, par=par, is_a=is_a, h=h):
                    bS = kb % 2
                    if is_a:
                        S.add("pe", lambda e: e.matmul(PS[bS][:, :], lhsT=KT[par][:, kb * 128:(kb + 1) * 128], rhs=QT[par], start=True, stop=True),
                              reads=r_QT[par] + r_KT[par], writes=[("PS", bS)])
                    else:
                        def emit(e):
                            e.matmul(PS[bS][:, :], lhsT=KT[par][:, kb * 128:(kb + 1) * 128], rhs=QT[par], start=True, stop=False)
                            return e.matmul(PS[bS][:, :], lhsT=identb[:, :], rhs=GB[par][:, kb, :], start=False, stop=True)
                        S.add("pe", emit, reads=r_QT[par] + r_KT[par] + r_GB[par] + ["identb"], writes=[("PS", bS)])
                s_op(0); s_op(1)
                flush_defer()
                for kb in range(nkb):
                    bS = kb % 2
                    S.add("act", lambda e, kb=kb, bS=bS, h=h: e.activation(out=EB[kb % 2], in_=PS[bS][:, :], func=AF.Exp,
                                                                         bias=NEGM[:, h:h + 1], scale=1.0),
                          reads=[("PS", bS), "NEGM"], writes=r_EB[kb % 2])
                    if is_a:
                        S.add("dve", lambda e, kb=kb, ti=ti: e.scalar_tensor_tensor(
                            out=PB[kb % 2], in0=EB[kb % 2], scalar=VFL[:, ti * 20 + kb:ti * 20 + kb + 1], in1=MA(kb),
                            op0=ALU.mult, op1=ALU.mult),
                            reads=r_EB[kb % 2] + r_mask + ["VFL"], writes=r_PB[kb % 2])
                    else:
                        S.add("dve", lambda e, kb=kb: e.tensor_tensor(out=PB[kb % 2], in0=EB[kb % 2], in1=MBv(kb), op=ALU.mult),
                              reads=r_EB[kb % 2] + r_mask, writes=r_PB[kb % 2])

                    def emit(e, kb=kb, par=par, bO=bO, bD=bD, nkb=nkb):
                        e.matmul(PS[bO][:, :], lhsT=VB[par][:, kb, :], rhs=PB[kb % 2], start=(kb == 0), stop=(kb == nkb - 1))
                        return e.matmul(PS[bD][:, :], lhsT=onesb[:, :], rhs=PB[kb % 2], start=(kb == 0), stop=(kb == nkb - 1))
                    S.add("pe", emit, reads=r_PB[kb % 2] + r_VB[par] + ["onesb"], writes=[("PS", bO), ("PS", bD)])
                    if kb + 2 < nkb:
                        s_op(kb + 2)
                S.add("dve", lambda e, bD=bD: e.reciprocal(out=RD[:, :], in_=PS[bD][:, :]), reads=[("PS", bD)], writes=["RD"])
                S.add("dve", lambda e, bO=bO, h=h: e.tensor_tensor(out=Hs[:, h, :], in0=PS[bO][:, :], in1=RD[:, :], op=ALU.mult),
                      reads=[("PS", bO), "RD"], writes=[r_H(h)])
                S.add("act", lambda e, h=h: e.activation(out=SQ[h % 2][:, :], in_=Hs[:, h, :], func=AF.Square),
                      reads=[r_H(h)], writes=[("SQ", h % 2)])
                first = (h == 0) or (h == HA)
                last = (h == HA - 1) or (h == NH - 1)

                def dss(h=h, bSS=bSS, first=first, last=last):
                    S.add("pe", lambda e: e.matmul(PS[bSS][:, :], lhsT=onesb[:, :], rhs=SQ[h % 2][:, :], start=first, stop=last),
                          reads=[("SQ", h % 2), "onesb"], writes=[("PS", bSS)])
                pe_defer.append(dss)
            flush_defer()
            for (rs_t, rn, bSS, wdt) in ((RS, "RS", 6, HA * 128), (RS2, "RS2", 7, HB * 128)):
                S.add("dve", lambda e, rs_t=rs_t, bSS=bSS, wdt=wdt: e.tensor_scalar(
                    out=rs_t[:, :], in0=PS[bSS][:, :], scalar1=1.0 / wdt, scalar2=EPS, op0=ALU.mult, op1=ALU.add),
                    reads=[("PS", bSS)], writes=[rn])
                S.add("act", lambda e, rs_t=rs_t: e.activation(out=rs_t[:, :], in_=rs_t[:, :], func=AF.Sqrt), reads=[rn], writes=[rn])
                S.add("dve", lambda e, rs_t=rs_t: e.reciprocal(out=rs_t[:, :], in_=rs_t[:, :]), reads=[rn], writes=[rn])
            for k in range(KC):
                rs_t = RS if k < HA else RS2
                rn = "RS" if k < HA else "RS2"
                S.add("dve", lambda e, k=k, rs_t=rs_t: e.scalar_tensor_tensor(
                    out=Xs[:, k, :], in0=Hs[:, k, :], scalar=gains[:, 2 * KC + k:2 * KC + k + 1], in1=rs_t[:, :],
                    op0=ALU.mult, op1=ALU.mult),
                    reads=[r_H(k), rn, "gains"], writes=[r_X(k)])
            S.add("sp", lambda e, ti=ti: e.dma_start(out=Hs[:, :, :].rearrange("p k t -> p (k t)"), in_=hscr[ti]),
                  reads=[("hscr", ti)], writes=[r_H(k) for k in range(KC)], chan="hld")
            xr = [r_X(k) for k in range(KC)]
            for pr in range(KC // 2):
                s = wload(wo[pr], 2 * KC * 128)
                wview = Wr[s][:, 0:2 * KC * 128].rearrange("p (j k c) -> p j k c", j=2, k=KC)
                for j in range(2):
                    m = pr * 2 + j
                    b = gbank()
                    mm_group(b, KC, lambda k, wview=wview, j=j: wview[:, j, k, :], lambda k: Xs[:, k, :], reads=[("W", s)] + xr)
                    S.add("dve", lambda e, b=b, m=m: e.tensor_tensor(out=Hs[:, m, :], in0=PS[b][:, :], in1=Hs[:, m, :], op=ALU.add),
                          reads=[("PS", b), r_H(m)], writes=[r_H(m)])
            norm(3)
            ffn(1)
            norm(4, final=True)
            for tb in range(4):
                sl = tb % 2
                for kq in range(KC // 4):
                    b = gbank()

                    def emit(e, tb=tb, kq=kq, b=b):
                        ins = None
                        for j in range(4):
                            ins = e.transpose(PS[b][:, j * 128:(j + 1) * 128], Hs[:, kq * 4 + j, tb * 128:(tb + 1) * 128], ident[:, :])
                        return ins
                    S.add("pe", emit, reads=[r_H(kq * 4 + j) for j in range(4)] + ["ident"], writes=[("PS", b)])
                    rr = ares(sl * 2 * D + kq * 1024, 1024)
                    if kq % 2 == 0:
                        S.add("dve", lambda e, sl=sl, kq=kq, b=b: e.tensor_copy(out=XS[sl][:, kq * 512:(kq + 1) * 512], in_=PS[b][:, :]),
                              reads=[("PS", b)], writes=rr)
                    else:
                        S.add("act", lambda e, sl=sl, kq=kq, b=b: e.activation(out=XS[sl][:, kq * 512:(kq + 1) * 512], in_=PS[b][:, :], func=AF.Copy),
                              reads=[("PS", b)], writes=rr)
                r0 = ti * TT + tb * 128
                S.add("sp", lambda e, sl=sl, r0=r0: e.dma_start(out=y_d[r0:r0 + 128, :], in_=XS[sl]),
                      reads=r_xs(sl), writes=[("y", r0)], chan=("XS", sl))

        engs = ["pe", "act", "dve", "pool", "sp"]
        chans = sorted(S.chan_count.keys(), key=str)
        sem_e = {e: es.enter_context(nc.semaphore(f"e_{e}")) for e in ["pe", "act", "dve", "pool"] if S.eng_count.get(e)}
        sem_c = {c: es.enter_context(nc.semaphore(f"c{i}")) for i, c in enumerate(chans)}

        def run_engine(eng_name, e):
            waited = {}
            for op in S.ops:
                if op.eng != eng_name:
                    continue
                need = {}
                for d in op.deps:
                    if d.chan is None:
                        if d.eng == eng_name and eng_name == "pe":
                            continue
                        key = ("e", d.eng)
                        sem = sem_e[d.eng]
                    else:
                        key = ("c", d.chan)
                        sem = sem_c[d.chan]
                    if need.get(key, (None, 0))[1] < d.idx:
                        need[key] = (sem, d.idx)
                for key, (sem, val) in need.items():
                    if waited.get(key, 0) >= val:
                        continue
                    e.wait_ge(sem, val)
                    waited[key] = val
                ins = op.emit(e)
                if op.chan is None:
                    ins.then_inc(sem_e[op.eng], 1)
                else:
                    ins.then_inc(sem_c[op.chan], 16)
            if eng_name == "sp":
                for c in chans:
                    e.wait_ge(sem_c[c], S.chan_count[c])

        with nc.Block() as block:
            @block.tensor
            def _(e):
                run_engine("pe", e)

            @block.scalar
            def _(e):
                run_engine("act", e)

            @block.vector
            def _(e):
                run_engine("dve", e)

            @block.gpsimd
            def _(e):
                run_engine("pool", e)

            @block.sync
            def _(e):
                run_engine("sp", e)
    return nc


def _structure(cfg):
    seq_id = np.concatenate([np.full(L, i, np.int64) for i, L in enumerate(cfg.SEQS)])
    pos = np.concatenate([np.arange(L, dtype=np.int64) for L in cfg.SEQS])
    slen = np.concatenate([np.full(L, L, np.int64) for L in cfg.SEQS])
    return seq_id, pos, slen


def host_prepare(cfg, inp):
    D, KC, FC, FG, HA, HB, NH = cfg.D, cfg.KC, cfg.FC, cfg.FG, cfg.HA, cfg.HB, cfg.NH
    OWN, EXT, NTO, HT = cfg.OWN, cfg.EXT, cfg.NTO, cfg.HT
    WA, WB = HA * 128, HB * 128
    f32 = np.float32
    stream = np.concatenate([np.asarray(inp["x_prompt"], f32).reshape(-1, D), np.asarray(inp["x_sample"], f32).reshape(-1, D)], 0)
    NTOK = stream.shape[0]
    seq_id, pos, slen = _structure(cfg)

    def gu_layout(wg, wu):
        a = np.empty((FC, 128, 2, KC, 128), f32)
        a[:, :, 0] = np.asarray(wg, f32).reshape(KC, 128, FC, 128).transpose(2, 1, 0, 3)
        a[:, :, 1] = np.asarray(wu, f32).reshape(KC, 128, FC, 128).transpose(2, 1, 0, 3)
        return a.reshape(FC, 128, 2 * KC * 128)

    def d_layout(wdn):
        a = np.asarray(wdn, f32).reshape(2, FG, 128, KC, 128).transpose(0, 3, 2, 1, 4)
        return np.ascontiguousarray(a).reshape(2 * KC, 128, FG * 128)

    def cols_layout_pairs(w, col_starts):
        n = len(col_starts) // 2
        a = np.empty((n, 128, 2, KC, 128), f32)
        w = np.asarray(w, f32)
        for i, c0 in enumerate(col_starts):
            a[i // 2, :, i % 2] = w[:, c0:c0 + 128].reshape(KC, 128, 128).transpose(1, 0, 2)
        return a.reshape(n, 128, 2 * KC * 128)

    w_in = np.asarray(inp["w_in"], f32)[0]
    kcols = [WA + h * 128 for h in range(HA)] + [3 * WA + WB + h * 128 for h in range(HB)]
    qcols = [h * 128 for h in range(HA)] + [3 * WA + h * 128 for h in range(HB)]
    wqk = cols_layout_pairs(w_in, kcols + qcols)
    vcols = [2 * WA + h * 128 for h in range(HA)] + [3 * WA + 2 * WB + h * 128 for h in range(HB)]
    wv = np.empty((NH // 2, 128, KC, 256), f32)
    for i, c0 in enumerate(vcols):
        wv[i // 2, :, :, (i % 2) * 128:(i % 2 + 1) * 128] = w_in[:, c0:c0 + 128].reshape(KC, 128, 128).transpose(1, 0, 2)
    wv = wv.reshape(NH // 2, 128, KC * 256)
    wo = cols_layout_pairs(np.asarray(inp["w_out"], f32)[0], [m * 128 for m in range(KC)])

    def gl(v):
        return np.asarray(v, f32).reshape(KC, 128).T
    gains = np.concatenate([
        gl(inp["ffn1_norm"][0]), gl(inp["mix_norm"][0]),
        gl(np.concatenate([np.asarray(inp["out_norm_a"][0], f32), np.asarray(inp["out_norm_b"][0], f32)])),
        gl(inp["ffn2_norm"][0]), gl(inp["final_norm"])], 1)
    gains = np.ascontiguousarray(gains, f32)

    consts = np.zeros((128, 256), f32)
    consts[:, 0:128] = np.eye(128, dtype=f32)
    for d in range(16):
        consts[d + 16, 128 + d] = -1.0
        consts[d, 128 + d + 16] = 1.0

    p = np.arange(128)[:, None]
    f = np.arange(TT)[None, :]
    maskA = np.zeros((128, 20, TT), f32)
    for kb in range(20):
        dl = -1024 + kb * 128 + p - f
        ad = np.abs(dl)
        maskA[:, kb, :] = ((ad <= 64).astype(f32) + ((dl % 4 == 0) & (ad <= 256)).astype(f32)
                           + ((dl % 16 == 0) & (ad <= 1024)).astype(f32))
    maskA = maskA.reshape(128, 20 * TT)

    rel = np.asarray(inp["nbr_rel_bias"], f32)[0]
    gb = np.zeros((HB, 128, 8, TT), f32)
    for kb in range(8):
        jrel = -256 + kb * 128 + p
        drow = (jrel // 64) - (f // 64)
        dr = drow + 7
        dc = np.clip((jrel % 64) - (f % 64), -15, 15) + 15
        ok = (dr >= 0) & (dr <= 14)
        g = rel[:, np.clip(dr, 0, 14), dc]
        gb[:, :, kb, :] = np.where(ok[None], g, 0.0)
    gb = gb.reshape(HB, 128, 8 * TT)
    relb = np.ascontiguousarray(rel.reshape(HB, 465))

    inv = (ROPE_THETA ** (-np.arange(0, 32, 2, dtype=f32) / f32(32))).astype(f32)

    shared = dict(wgu1=gu_layout(inp["ffn1_w_gate"][0], inp["ffn1_w_up"][0]), wd1=d_layout(inp["ffn1_w_down"][0]),
                  wgu2=gu_layout(inp["ffn2_w_gate"][0], inp["ffn2_w_up"][0]), wd2=d_layout(inp["ffn2_w_down"][0]),
                  wqk=wqk, wv=wv, wo=wo, gains=gains, consts=consts, relb=relb, gb=gb, maskA=maskA)
    in_maps = []
    for c in range(cfg.NCORES):
        g0 = c * OWN - HALO
        gidx = np.arange(g0, g0 + EXT)
        valid = (gidx >= 0) & (gidx < NTOK)
        gcl = np.clip(gidx, 0, NTOK - 1)
        x_ext = np.where(valid[:, None], stream[gcl], f32(0.0)).astype(f32)
        e_seq = np.where(valid, seq_id[gcl], -1)
        e_pos = np.where(valid, pos[gcl], 0)
        e_len = np.where(valid, slen[gcl], 64 * 8)
        ang = (e_pos.astype(f32)[None, :] * inv[:, None]).astype(f32)
        ropeC = np.ones((128, EXT), f32); ropeS = np.zeros((128, EXT), f32)
        ropeC[0:16] = np.cos(ang); ropeC[16:32] = np.cos(ang)
        ropeS[0:16] = np.sin(ang); ropeS[16:32] = np.sin(ang)
        vflag = np.zeros((NTO, 20), f32)
        maskB = np.zeros((NTO, 128, 8, TT), f32)
        for ti in range(NTO):
            i0 = (ti + HT) * TT
            qs = e_seq[i0]
            for kb in range(20):
                j0 = i0 - 1024 + kb * 128
                vflag[ti, kb] = 1.0 if (e_seq[j0] == qs and qs >= 0) else 0.0
            qpos = e_pos[i0:i0 + TT]; rows = e_len[i0] // 64
            r = qpos // 64; cc = qpos % 64
            rs = np.clip(r - 4, 0, rows - 8); cs = np.clip(cc - 8, 0, 64 - 16)
            for kb in range(8):
                j0 = i0 - 256 + kb * 128
                kp = e_pos[j0:j0 + 128]; ks = e_seq[j0:j0 + 128]
                rho = kp // 64; gam = kp % 64
                ok = ((ks[:, None] == qs) & (qs >= 0) & (rho[:, None] >= rs[None, :]) & (rho[:, None] < rs[None, :] + 8)
                      & (gam[:, None] >= cs[None, :]) & (gam[:, None] < cs[None, :] + 16))
                maskB[ti, :, kb, :] = ok.astype(f32)
        m = dict(shared)
        m.update(x_ext=x_ext, ropeC=ropeC, ropeS=ropeS,
                 vflag=np.ascontiguousarray(np.broadcast_to(vflag.reshape(1, NTO * 20), (128, NTO * 20)), f32),
                 maskB=maskB.reshape(NTO, 128, 8 * TT))
        in_maps.append(m)
    return in_maps


def run(cfg, inp):
    in_maps = host_prepare(cfg, inp)
    nc = build_program(cfg)
    res = run_bass_kernel_spmd(nc, in_maps, core_ids=list(range(cfg.NCORES)))
    ys = np.concatenate([np.asarray(r["y"], np.float32) for r in res.results], 0)
    return ys


def kernel(x_prompt, x_sample, **w):
    cfg = Cfg()
    inp = dict(w)
    inp["x_prompt"] = x_prompt
    inp["x_sample"] = x_sample
    ys = run(cfg, inp)
    D = cfg.D
    npr = x_prompt.shape[0] * x_prompt.shape[1]
    y_prompt = ys[:npr].reshape(x_prompt.shape).astype(np.float32)
    y_sample = ys[npr:].reshape(x_sample.shape).astype(np.float32)
    return (y_prompt, y_sample)
```

```python
import numpy as np
import concourse.bass as bass
import concourse.mybir as mybir
from concourse.bass_utils import run_bass_kernel_spmd

F32 = mybir.dt.float32
BF16 = mybir.dt.bfloat16
AF = mybir.ActivationFunctionType
ALU = mybir.AluOpType
AX = mybir.AxisListType

HALO = 1024
TT = 512
ROPE_THETA = 500000.0
EPS = 1e-6


class Cfg:
    def __init__(self, D=4096, DFF=11008, HA=16, HB=16, NCORES=8, OWN=3072,
                 SEQS=(8192, 8192, 4096, 4096)):
        self.D, self.DFF, self.HA, self.HB = D, DFF, HA, HB
        self.NCORES, self.OWN, self.SEQS = NCORES, OWN, tuple(SEQS)
        self.KC = D // 128
        self.FC = DFF // 128
        self.FG = self.FC // 2
        self.NH = HA + HB
        self.EXT = OWN + 2 * HALO
        self.NTE = self.EXT // TT
        self.NTO = OWN // TT
        self.HT = HALO // TT
        assert self.FC % 2 == 0 and HA % 2 == 0 and HB % 2 == 0
        assert (HA + HB) * 128 == D and self.KC % 4 == 0
        assert sum(SEQS) == NCORES * OWN
        self.WSLOT = max(self.KC * 128, self.FG * 128)
        self.ABYTES = max(self.FG * TT * 2, 43008, 2 * D * 4)
        self.BIGW = self.WSLOT * 4 >= 28 * TT


class Op:
    __slots__ = ("eng", "emit", "deps", "chan", "idx")


class Sched:
    def __init__(self):
        self.ops = []
        self.lastw = {}
        self.readers = {}
        self.eng_count = {}
        self.chan_count = {}

    def add(self, eng, emit, reads=(), writes=(), chan=None):
        op = Op()
        op.eng, op.emit, op.chan = eng, emit, chan
        deps = set()
        for r in reads:
            w = self.lastw.get(r)
            if w is not None:
                deps.add(w)
        for w_ in writes:
            w = self.lastw.get(w_)
            if w is not None:
                deps.add(w)
            rs = self.readers.get(w_)
            if rs:
                deps.update(rs)
        op.deps = deps
        for r in reads:
            self.readers.setdefault(r, []).append(op)
        for w_ in writes:
            self.lastw[w_] = op
            self.readers[w_] = []
        if chan is None:
            self.eng_count[eng] = self.eng_count.get(eng, 0) + 1
            op.idx = self.eng_count[eng]
        else:
            self.chan_count[chan] = self.chan_count.get(chan, 0) + 16
            op.idx = self.chan_count[chan]
        self.ops.append(op)
        return op


def build_program(cfg):
    D, KC, FC, FG = cfg.D, cfg.KC, cfg.FC, cfg.FG
    HA, HB, NH = cfg.HA, cfg.HB, cfg.NH
    OWN, EXT, NTE, NTO, HT = cfg.OWN, cfg.EXT, cfg.NTE, cfg.NTO, cfg.HT
    NBLK = EXT // 128
    WSLOT = cfg.WSLOT
    NBW = 4
    SCALE = 128.0 ** -0.5

    nc = bass.Bass("TRN2", target_bir_lowering=False)

    def din(name, shape, dt=F32):
        return nc.dram_tensor(name, list(shape), dt, kind="ExternalInput").ap()

    x_ext = din("x_ext", [EXT, D])
    wg = [din("wg1", [FC, 128, KC * 128]), din("wg2", [FC, 128, KC * 128])]
    wu = [din("wu1", [FC, 128, KC * 128]), din("wu2", [FC, 128, KC * 128])]
    wd = [din("wd1", [2 * KC, 128, FG * 128]), din("wd2", [2 * KC, 128, FG * 128])]
    wqk = din("wqk", [2 * NH, 128, KC * 128])
    wv = din("wv", [NH, 128, KC * 128])
    wo = din("wo", [KC, 128, KC * 128])
    gains_d = din("gains", [128, 5 * KC])
    ropeC_d = din("ropeC", [128, EXT])
    ropeS_d = din("ropeS", [128, EXT])
    consts_d = din("consts", [128, 256])
    relb_d = din("relb", [HB, 465])
    gb_d = din("gb", [HB, 128, 8 * TT])
    maskA_d = din("maskA", [128, 20 * TT])
    vflag_d = din("vflag", [128, NTO * 20])
    maskB_d = din("maskB", [NTO, 128, 8 * TT])
    y_d = nc.dram_tensor("y", [OWN, D], F32, kind="ExternalOutput").ap()
    hscr = nc.dram_tensor("hscr", [NTO, 128, KC * TT], F32, kind="Internal").ap()
    qscr = nc.dram_tensor("qscr", [NH, 128, OWN], BF16, kind="Internal").ap()
    kscr = nc.dram_tensor("kscr", [NH, 128, EXT], BF16, kind="Internal").ap()
    vscr = nc.dram_tensor("vscr", [NH, 128, NBLK * 128], BF16, kind="Internal").ap()

    S = Sched()
    import contextlib
    es = contextlib.ExitStack()
    with es:
        def sb(name, shape, dt):
            return es.enter_context(nc.sbuf_tensor("s_" + name, list(shape), dt))

        Hs = sb("H", [128, KC, TT], F32)
        Xs = sb("X", [128, KC, TT], BF16)
        Ar = sb("A", [128, cfg.ABYTES // 2], BF16)
        Wall = sb("Wall", [128, NBW * WSLOT], BF16)
        Wr = [Wall[:, i * WSLOT:(i + 1) * WSLOT] for i in range(NBW)]
        ident = sb("ident", [128, 128], F32)
        piT = sb("piT", [128, 128], F32)
        identb = sb("identb", [128, 128], BF16)
        onesb = sb("onesb", [128, 128], BF16)
        gains = sb("gains", [128, 5 * KC], F32)
        RS = sb("RS", [128, TT], F32)
        SQ = [sb(f"SQ{i}", [128, TT], BF16) for i in range(2)]
        TS = [sb(f"TS{i}", [128, TT], BF16) for i in range(2)]
        CT = sb("CT", [128, TT], F32)
        ST = sb("ST", [128, TT], F32)
        QF = [sb(f"QF{i}", [128, TT], F32) for i in range(2)]
        T1 = sb("T1", [128, TT], F32)
        T2 = sb("T2", [128, TT], F32)
        RS2 = T2
        OUTB = TS
        VO = [sb(f"VO{i}", [128, TT], BF16) for i in range(2)]
        KMAX = sb("KMAX", [128, NH], F32)
        QMAX = sb("QMAX", [128, NH], F32)
        NEGM = sb("NEGM", [128, NH], F32)
        BSS = sb("BSS", [128, HB], F32)
        TM = [sb(f"TM{i}", [128, 1], F32) for i in range(2)]
        VFL = sb("VFL", [128, NTO * 20], F32)
        RD = T1
        EB2 = VO[0]
        PB2 = VO[1]
        PS = [es.enter_context(nc.psum_tensor(f"ps{i}", [128, TT], F32)) for i in range(8)]

        A_act = Ar[:, 0:FG * TT].rearrange("p (f t) -> p f t", t=TT)
        XS = [Ar[:, i * 2 * D:(i + 1) * 2 * D].bitcast(F32) for i in range(2)]
        o = 0
        QT = []; KT = []; VB = []; GB = []; EB = []; PB = []
        for i in range(2):
            QT.append(Ar[:, o:o + TT]); o += TT
        for i in range(2):
            KT.append(Ar[:, o:o + 2560]); o += 2560
        for i in range(2):
            VB.append(Ar[:, o:o + 2560].rearrange("p (b d) -> p b d", d=128)); o += 2560
        for i in range(2):
            GB.append(Ar[:, o:o + 8 * TT].rearrange("p (b t) -> p b t", t=TT)); o += 8 * TT
        for i in range(2):
            EB.append(Ar[:, o:o + TT]); o += TT
        for i in range(2):
            PB.append(Ar[:, o:o + TT]); o += TT
        assert o * 2 <= cfg.ABYTES
        A_RES = 1024

        def ares(lo_el, n_el):
            lo = lo_el * 2; hi = (lo_el + n_el) * 2
            return [("A", i) for i in range(lo // A_RES, (hi + A_RES - 1) // A_RES)]

        r_act = lambda f: ares(f * TT, TT)
        r_xs = lambda i: ares(i * 2 * D, 2 * D)
        _o = [0]

        def _nx(n):
            r = ares(_o[0], n); _o[0] += n; return r
        r_QT = [_nx(TT) for i in range(2)]
        r_KT = [_nx(2560) for i in range(2)]
        r_VB = [_nx(2560) for i in range(2)]
        r_GB = [_nx(8 * TT) for i in range(2)]
        r_EB = [_nx(TT) for i in range(2)]
        r_PB = [_nx(TT) for i in range(2)]
        if cfg.BIGW:
            MA = lambda kb: Wall[:, kb * TT:(kb + 1) * TT]
            MBv = lambda kb: Wall[:, (20 + kb) * TT:(21 + kb) * TT]
            ma_dst = [(Wall[:, 0:20 * TT], 0, 20)]
            mb_dst = Wall[:, 20 * TT:28 * TT]
            r_mask = [("W", i) for i in range((28 * TT + WSLOT - 1) // WSLOT)]
        else:
            MAt = sb("MAt", [128, 20 * TT], BF16)
            MBt = sb("MBt", [128, 8 * TT], BF16)
            MA = lambda kb: MAt[:, kb * TT:(kb + 1) * TT]
            MBv = lambda kb: MBt[:, kb * TT:(kb + 1) * TT]
            ma_dst = [(MAt[:, :], 0, 20)]
            mb_dst = MBt[:, :]
            r_mask = [("MASK",)]

        bank_rr = [0]

        def gbank():
            b = bank_rr[0] % 6
            bank_rr[0] += 1
            return b

        wslot_rr = [0]

        def wload(src_ap, nel):
            s = wslot_rr[0] % NBW
            wslot_rr[0] += 1
            S.add("pool", lambda e, s=s, src_ap=src_ap, nel=nel: e.dma_start(out=Wr[s][:, 0:nel], in_=src_ap),
                  writes=[("W", s)], chan=("W", s))
            return s

        pe_defer = []

        def flush_defer():
            while pe_defer:
                f = pe_defer.pop(0)
                f()

        def mm_group(bank, nmm, lhs_fn, rhs_fn, reads, out_ap=None):
            outp = PS[bank][:, :] if out_ap is None else out_ap

            def emit(e):
                ins = None
                for k in range(nmm):
                    ins = e.matmul(outp, lhsT=lhs_fn(k), rhs=rhs_fn(k), start=(k == 0), stop=(k == nmm - 1))
                return ins
            S.add("pe", emit, reads=reads, writes=[("PS", bank)])

        S.add("sp", lambda e: e.dma_start(out=ident[:, :], in_=consts_d[:, 0:128]), writes=["ident"], chan="c_id")
        S.add("sp", lambda e: e.dma_start(out=piT[:, :], in_=consts_d[:, 128:256]), writes=["piT"], chan="c_pi")
        S.add("sp", lambda e: e.dma_start(out=gains[:, :], in_=gains_d[:, :]), writes=["gains"], chan="c_g")
        S.add("sp", lambda e: e.dma_start(out=VFL[:, :], in_=vflag_d[:, :]), writes=["VFL"], chan="c_vf")
        S.add("dve", lambda e: e.memset(onesb[:, :], 1.0), writes=["onesb"])
        S.add("dve", lambda e: e.tensor_copy(out=identb[:, :], in_=ident[:, :]), reads=["ident"], writes=["identb"])
        S.add("dve", lambda e: e.memset(KMAX[:, :], 0.0), writes=["KMAX"])
        S.add("dve", lambda e: e.memset(QMAX[:, :], 0.0), writes=["QMAX"])
        S.add("dve", lambda e: e.memset(BSS[:, :], 0.0), writes=["BSS"])

        r_H = lambda k: ("H", k)
        r_X = lambda k: ("X", k)

        def norm(gidx, final=False):
            for k in range(KC):
                S.add("act", lambda e, k=k: e.activation(out=SQ[k % 2][:, :], in_=Hs[:, k, :], func=AF.Square),
                      reads=[r_H(k)], writes=[("SQ", k % 2)])
                S.add("pe", lambda e, k=k: e.matmul(PS[6][:, :], lhsT=onesb[:, :], rhs=SQ[k % 2][:, :],
                                                   start=(k == 0), stop=(k == KC - 1)),
                      reads=[("SQ", k % 2), "onesb"], writes=[("PS", 6)])
            S.add("dve", lambda e: e.tensor_scalar(out=RS[:, :], in0=PS[6][:, :], scalar1=1.0 / D, scalar2=EPS,
                                                   op0=ALU.mult, op1=ALU.add),
                  reads=[("PS", 6)], writes=["RS"])
            S.add("act", lambda e: e.activation(out=RS[:, :], in_=RS[:, :], func=AF.Sqrt), reads=["RS"], writes=["RS"])
            S.add("dve", lambda e: e.reciprocal(out=RS[:, :], in_=RS[:, :]), reads=["RS"], writes=["RS"])
            for k in range(KC):
                if final:
                    S.add("dve", lambda e, k=k: e.scalar_tensor_tensor(
                        out=Hs[:, k, :], in0=Hs[:, k, :], scalar=gains[:, gidx * KC + k:gidx * KC + k + 1],
                        in1=RS[:, :], op0=ALU.mult, op1=ALU.mult),
                        reads=[r_H(k), "RS", "gains"], writes=[r_H(k)])
                else:
                    S.add("dve", lambda e, k=k: e.scalar_tensor_tensor(
                        out=Xs[:, k, :], in0=Hs[:, k, :], scalar=gains[:, gidx * KC + k:gidx * KC + k + 1],
                        in1=RS[:, :], op0=ALU.mult, op1=ALU.mult),
                        reads=[r_H(k), "RS", "gains"], writes=[r_X(k)])

        def ffn(li):
            xr = [r_X(k) for k in range(KC)]
            for g in range(2):
                for fl in range(FG):
                    f = g * FG + fl
                    s = wload(wg[li][f], KC * 128)
                    wg_ = Wr[s][:, 0:KC * 128].rearrange("p (k c) -> p k c", k=KC)
                    s2 = wload(wu[li][f], KC * 128)
                    wu_ = Wr[s2][:, 0:KC * 128].rearrange("p (k c) -> p k c", k=KC)
                    bg = gbank(); bu = gbank()
                    mm_group(bg, KC, lambda k, wg_=wg_: wg_[:, k, :], lambda k: Xs[:, k, :], reads=[("W", s)] + xr)
                    mm_group(bu, KC, lambda k, wu_=wu_: wu_[:, k, :], lambda k: Xs[:, k, :], reads=[("W", s2)] + xr)
                    flush_defer()
                    S.add("act", lambda e, bg=bg, fl=fl: e.activation(out=TS[fl % 2][:, :], in_=PS[bg][:, :], func=AF.Silu),
                          reads=[("PS", bg)], writes=[("TS", fl % 2)])
                    S.add("dve", lambda e, bu=bu, fl=fl: e.tensor_tensor(out=A_act[:, fl, :], in0=TS[fl % 2][:, :],
                                                                       in1=PS[bu][:, :], op=ALU.mult),
                          reads=[("PS", bu), ("TS", fl % 2)], writes=r_act(fl))
                ar = [r for fl in range(FG) for r in r_act(fl)]
                for m in range(KC):
                    s = wload(wd[li][g * KC + m], FG * 128)
                    wv_ = Wr[s][:, 0:FG * 128].rearrange("p (k c) -> p k c", k=FG)
                    b = gbank()
                    mm_group(b, FG, lambda k, wv_=wv_: wv_[:, k, :], lambda k: A_act[:, k, :], reads=[("W", s)] + ar)
                    S.add("dve", lambda e, b=b, m=m: e.scalar_tensor_tensor(
                        out=Hs[:, m, :], in0=PS[b][:, :], scalar=0.5, in1=Hs[:, m, :], op0=ALU.mult, op1=ALU.add),
                        reads=[("PS", b), r_H(m)], writes=[r_H(m)])

        def qk_chunk(t, idx, h, is_q, is_a, wview, s, j):
            xr = [r_X(k) for k in range(KC)]
            b = gbank()
            mm_group(b, KC, lambda k: wview[:, k, :], lambda k: Xs[:, k, :], reads=[("W", s)] + xr)
            flush_defer()
            ob = idx % 2
            sc = SCALE if is_q else 1.0
            if is_a:
                S.add("act", lambda e: e.activation(out=QF[ob][:, :], in_=PS[b][:, :], func=AF.Copy, scale=sc),
                      reads=[("PS", b)], writes=[("QF", ob)])

                def d1():
                    S.add("pe", lambda e: e.matmul(PS[7][:, :], lhsT=piT[:, :], rhs=QF[ob][:, :], start=True, stop=True),
                          reads=[("QF", ob), "piT"], writes=[("PS", 7)])
                    S.add("dve", lambda e: e.tensor_tensor(out=T1[:, :], in0=QF[ob][:, :], in1=CT[:, :], op=ALU.mult),
                          reads=[("QF", ob), "CT"], writes=["T1"])
                    S.add("dve", lambda e: e.tensor_tensor(out=T2[:, :], in0=PS[7][:, :], in1=ST[:, :], op=ALU.mult),
                          reads=[("PS", 7), "ST"], writes=["T2"])
                    S.add("dve", lambda e: e.tensor_tensor(out=OUTB[ob][:, :], in0=T1[:, :], in1=T2[:, :], op=ALU.add),
                          reads=["T1", "T2"], writes=[("TS", ob)])
                    tail()
                pe_defer.append(d1)
            else:
                S.add("act", lambda e: e.activation(out=OUTB[ob][:, :], in_=PS[b][:, :], func=AF.Copy, scale=sc),
                      reads=[("PS", b)], writes=[("TS", ob)])
                pe_defer.append(lambda: tail())

            def tail():
                S.add("act", lambda e: e.activation(out=SQ[ob][:, :], in_=OUTB[ob][:, :], func=AF.Square),
                      reads=[("TS", ob)], writes=[("SQ", ob)])
                S.add("pe", lambda e: e.matmul(PS[6][:, :], lhsT=onesb[:, :], rhs=SQ[ob][:, :], start=True, stop=True),
                      reads=[("SQ", ob), "onesb"], writes=[("PS", 6)])
                S.add("dve", lambda e: e.tensor_reduce(out=TM[ob][:, :], in_=PS[6][:, :], axis=AX.X, op=ALU.max),
                      reads=[("PS", 6)], writes=[("TM", ob)])
                MX = QMAX if is_q else KMAX
                mxn = "QMAX" if is_q else "KMAX"
                S.add("dve", lambda e: e.tensor_tensor(out=MX[:, h:h + 1], in0=MX[:, h:h + 1], in1=TM[ob][:, :], op=ALU.max),
                      reads=[("TM", ob), mxn], writes=[mxn])
                if is_q:
                    to = t - HT
                    dst = qscr[h, :, to * TT:(to + 1) * TT]
                else:
                    dst = kscr[h, :, t * TT:(t + 1) * TT]
                S.add("sp", lambda e: e.dma_start(out=dst, in_=OUTB[ob][:, :]), reads=[("TS", ob)],
                      writes=[("scr", "q" if is_q else "k", h, t)], chan=("TS", ob))

        for t in range(NTE):
            own = HT <= t < HT + NTO
            S.add("sp", lambda e, t=t: e.dma_start(out=CT[:, :], in_=ropeC_d[:, t * TT:(t + 1) * TT]), writes=["CT"], chan="ropeC")
            S.add("sp", lambda e, t=t: e.dma_start(out=ST[:, :], in_=ropeS_d[:, t * TT:(t + 1) * TT]), writes=["ST"], chan="ropeS")
            for tb in range(4):
                sl = tb % 2
                r0 = t * TT + tb * 128
                S.add("sp", lambda e, sl=sl, r0=r0: e.dma_start(out=XS[sl], in_=x_ext[r0:r0 + 128, :]),
                      writes=r_xs(sl), chan=("XS", sl))
                for kq in range(KC // 4):
                    b = gbank()

                    def emit(e, sl=sl, kq=kq, b=b):
                        ins = None
                        for j in range(4):
                            ins = e.transpose(PS[b][:, j * 128:(j + 1) * 128], XS[sl][:, (kq * 4 + j) * 128:(kq * 4 + j + 1) * 128], ident[:, :])
                        return ins
                    S.add("pe", emit, reads=r_xs(sl) + ["ident"], writes=[("PS", b)])
                    eng = "dve" if kq % 2 == 0 else "act"
                    outv = Hs[:, kq * 4:(kq + 1) * 4, tb * 128:(tb + 1) * 128]
                    inv = PS[b][:, :].rearrange("p (a c) -> p a c", a=4)
                    if eng == "dve":
                        S.add("dve", lambda e, outv=outv, inv=inv: e.tensor_copy(out=outv, in_=inv),
                              reads=[("PS", b)], writes=[r_H(kq * 4 + j) for j in range(4)])
                    else:
                        S.add("act", lambda e, outv=outv, inv=inv: e.activation(out=outv, in_=inv, func=AF.Copy),
                              reads=[("PS", b)], writes=[r_H(kq * 4 + j) for j in range(4)])
            norm(0)
            ffn(0)
            norm(1)
            if own:
                S.add("sp", lambda e, t=t: e.dma_start(out=hscr[t - HT], in_=Hs[:, :, :].rearrange("p k t -> p (k t)")),
                      reads=[r_H(k) for k in range(KC)], writes=[("hscr", t - HT)], chan="hst")
            idx = 0
            far = (t == 0) or (t == NTE - 1)
            for h in range(NH):
                if far and h >= HA:
                    continue
                s = wload(wqk[h], KC * 128)
                wview = Wr[s][:, 0:KC * 128].rearrange("p (k c) -> p k c", k=KC)
                qk_chunk(t, idx, h, False, h < HA, wview, s, 0); idx += 1
            xr = [r_X(k) for k in range(KC)]
            vcnt = 0
            for h in range(NH):
                if far and h >= HA:
                    continue
                s = wload(wv[h], KC * 128)
                wview = Wr[s][:, 0:KC * 128].rearrange("p (k c) -> p k c", k=KC)
                b = gbank()
                for tb in range(4):
                    mm_group(b, KC, lambda k, tb=tb: Xs[:, k, tb * 128:(tb + 1) * 128], lambda k, wview=wview: wview[:, k, :],
                             reads=[("W", s)] + xr, out_ap=PS[b][:, tb * 128:(tb + 1) * 128])
                flush_defer()
                vb = vcnt % 2; vcnt += 1
                S.add("act", lambda e, b=b, vb=vb: e.activation(out=VO[vb][:, :], in_=PS[b][:, :], func=AF.Copy),
                      reads=[("PS", b)], writes=[("VO", vb)])
                S.add("sp", lambda e, h=h, vb=vb, t=t: e.dma_start(out=vscr[h, :, t * TT:(t + 1) * TT], in_=VO[vb][:, :]),
                      reads=[("VO", vb)], writes=[("scr", "v", h, t)], chan=("VO", vb))
            if own:
                for h in range(NH):
                    s = wload(wqk[NH + h], KC * 128)
                    wview = Wr[s][:, 0:KC * 128].rearrange("p (k c) -> p k c", k=KC)
                    qk_chunk(t, idx, h, True, h < HA, wview, s, 0); idx += 1
            flush_defer()

        for hb in range(HB):
            S.add("sp", lambda e, hb=hb: e.dma_start(out=T1[:, 0:465], in_=relb_d[hb:hb + 1, :].to_broadcast([128, 465])),
                  writes=["T1"], chan="relb")
            S.add("act", lambda e, hb=hb: e.activation(out=T2[:, 0:465], in_=T1[:, 0:465], func=AF.Square,
                                                       accum_out=BSS[:, hb:hb + 1]),
                  reads=["T1", "BSS"], writes=["T2", "BSS"])
        S.add("dve", lambda e: e.tensor_tensor(out=NEGM[:, :], in0=QMAX[:, :], in1=KMAX[:, :], op=ALU.mult),
              reads=["QMAX", "KMAX"], writes=["NEGM"])
        S.add("act", lambda e: e.activation(out=NEGM[:, :], in_=NEGM[:, :], func=AF.Sqrt), reads=["NEGM"], writes=["NEGM"])
        S.add("act", lambda e: e.activation(out=BSS[:, :], in_=BSS[:, :], func=AF.Sqrt), reads=["BSS"], writes=["BSS"])
        S.add("dve", lambda e: e.tensor_scalar(out=NEGM[:, :], in0=NEGM[:, :], scalar1=-1.02, scalar2=None, op0=ALU.mult),
              reads=["NEGM"], writes=["NEGM"])
        S.add("dve", lambda e: e.tensor_tensor(out=NEGM[:, HA:NH], in0=NEGM[:, HA:NH], in1=BSS[:, :], op=ALU.subtract),
              reads=["NEGM", "BSS"], writes=["NEGM"])

        def attn_loads(ti, h):
            par = h % 2
            i0 = (ti + HT) * TT
            is_a = h < HA
            klo = i0 - 1024 if is_a else i0 - 256
            nk = 2560 if is_a else 1024
            S.add("sp", lambda e: e.dma_start(out=QT[par], in_=qscr[h, :, ti * TT:(ti + 1) * TT]),
                  reads=[("scr", "q", h, ti + HT)], writes=r_QT[par], chan=("QT", par))
            S.add("sp", lambda e: e.dma_start(out=KT[par][:, 0:nk], in_=kscr[h, :, klo:klo + nk]),
                  reads=[("scr", "k", h, tt) for tt in range(klo // TT, (klo + nk - 1) // TT + 1)], writes=r_KT[par], chan=("KT", par))
            b0 = klo // 128
            nb = nk // 128
            S.add("sp", lambda e: e.dma_start(out=VB[par][:, 0:nb, :].rearrange("p b d -> p (b d)"), in_=vscr[h, :, b0 * 128:(b0 + nb) * 128]),
                  reads=[("scr", "v", h, tt) for tt in range(klo // TT, (klo + nk - 1) // TT + 1)], writes=r_VB[par], chan=("VB", par))
            if not is_a:
                S.add("pool", lambda e: e.dma_start(out=GB[par].rearrange("p b t -> p (b t)"), in_=gb_d[h - HA]),
                      writes=r_GB[par], chan=("GB", par))

        for ti in range(NTO):
            for (dst, k0, n) in ma_dst:
                S.add("pool", lambda e, dst=dst, k0=k0, n=n: e.dma_start(out=dst, in_=maskA_d[:, k0 * TT:(k0 + n) * TT]),
                      writes=r_mask, chan="MA")
            S.add("pool", lambda e, ti=ti: e.dma_start(out=mb_dst, in_=maskB_d[ti]), writes=r_mask, chan="MA")
            attn_loads(ti, 0)
            EBL = [EB[0], EB[1], EB2[:, :]]
            PBL = [PB[0], PB[1], PB2[:, :]]
            r_EBL = [r_EB[0], r_EB[1], [("VO", 0)]]
            r_PBL = [r_PB[0], r_PB[1], [("VO", 1)]]
            DEPTH = 3

            def rs_group(rs_t, rn, wdt):
                S.add("dve", lambda e: e.tensor_scalar(out=rs_t[:, :], in0=PS[7][:, :], scalar1=1.0 / wdt, scalar2=EPS,
                                                       op0=ALU.mult, op1=ALU.add), reads=[("PS", 7)], writes=[rn])
                S.add("act", lambda e: e.activation(out=rs_t[:, :], in_=rs_t[:, :], func=AF.Sqrt), reads=[rn], writes=[rn])
                S.add("dve", lambda e: e.reciprocal(out=rs_t[:, :], in_=rs_t[:, :]), reads=[rn], writes=[rn])

            for h in range(NH):
                if h + 1 < NH:
                    attn_loads(ti, h + 1)
                par = h % 2
                is_a = h < HA
                nkb = 20 if is_a else 8
                bO = 3 + par; bD = 5 + par

                def s_op(kb, par=par, is_a=is_a, h=h):
                    bS = kb % DEPTH
                    if is_a:
                        S.add("pe", lambda e: e.matmul(PS[bS][:, :], lhsT=KT[par][:, kb * 128:(kb + 1) * 128], rhs=QT[par], start=True, stop=True),
                              reads=r_QT[par] + r_KT[par], writes=[("PS", bS)])
                    else:
                        def emit(e):
                            e.matmul(PS[bS][:, :], lhsT=KT[par][:, kb * 128:(kb + 1) * 128], rhs=QT[par], start=True, stop=False)
                            return e.matmul(PS[bS][:, :], lhsT=identb[:, :], rhs=GB[par][:, kb, :], start=False, stop=True)
                        S.add("pe", emit, reads=r_QT[par] + r_KT[par] + r_GB[par] + ["identb"], writes=[("PS", bS)])
                for kb in range(DEPTH):
                    s_op(kb)
                flush_defer()
                if h == HA:
                    rs_group(RS, "RS", HA * 128)
                for kb in range(nkb):
                    bS = kb % DEPTH
                    eb = kb % DEPTH
                    S.add("act", lambda e, eb=eb, bS=bS, h=h: e.activation(out=EBL[eb], in_=PS[bS][:, :], func=AF.Exp,
                                                                         bias=NEGM[:, h:h + 1], scale=1.0),
                          reads=[("PS", bS), "NEGM"], writes=r_EBL[eb])
                    if is_a:
                        S.add("dve", lambda e, kb=kb, eb=eb, ti=ti: e.scalar_tensor_tensor(
                            out=PBL[eb], in0=EBL[eb], scalar=VFL[:, ti * 20 + kb:ti * 20 + kb + 1], in1=MA(kb),
                            op0=ALU.mult, op1=ALU.mult),
                            reads=r_EBL[eb] + r_mask + ["VFL"], writes=r_PBL[eb])
                    else:
                        S.add("dve", lambda e, kb=kb, eb=eb: e.tensor_tensor(out=PBL[eb], in0=EBL[eb], in1=MBv(kb), op=ALU.mult),
                              reads=r_EBL[eb] + r_mask, writes=r_PBL[eb])

                    def emit(e, kb=kb, eb=eb, par=par, bO=bO, bD=bD, nkb=nkb):
                        e.matmul(PS[bO][:, :], lhsT=VB[par][:, kb, :], rhs=PBL[eb], start=(kb == 0), stop=(kb == nkb - 1))
                        return e.matmul(PS[bD][:, :], lhsT=onesb[:, :], rhs=PBL[eb], start=(kb == 0), stop=(kb == nkb - 1))
                    S.add("pe", emit, reads=r_PBL[eb] + r_VB[par] + ["onesb"], writes=[("PS", bO), ("PS", bD)])
                    if kb + DEPTH < nkb:
                        s_op(kb + DEPTH)
                S.add("dve", lambda e, bD=bD: e.reciprocal(out=RD[:, :], in_=PS[bD][:, :]), reads=[("PS", bD)], writes=["T1"])
                S.add("dve", lambda e, bO=bO, h=h: e.tensor_tensor(out=Hs[:, h, :], in0=PS[bO][:, :], in1=RD[:, :], op=ALU.mult),
                      reads=[("PS", bO), "T1"], writes=[r_H(h)])
                S.add("act", lambda e, h=h: e.activation(out=SQ[h % 2][:, :], in_=Hs[:, h, :], func=AF.Square),
                      reads=[r_H(h)], writes=[("SQ", h % 2)])
                first = (h == 0) or (h == HA)
                last = (h == HA - 1) or (h == NH - 1)

                def dss(h=h, first=first, last=last):
                    S.add("pe", lambda e: e.matmul(PS[7][:, :], lhsT=onesb[:, :], rhs=SQ[h % 2][:, :], start=first, stop=last),
                          reads=[("SQ", h % 2), "onesb"], writes=[("PS", 7)])
                pe_defer.append(dss)
            flush_defer()
            rs_group(RS2, "T2", HB * 128)
            for k in range(KC):
                rs_t = RS if k < HA else RS2
                rn = "RS" if k < HA else "T2"
                S.add("dve", lambda e, k=k, rs_t=rs_t: e.scalar_tensor_tensor(
                    out=Xs[:, k, :], in0=Hs[:, k, :], scalar=gains[:, 2 * KC + k:2 * KC + k + 1], in1=rs_t[:, :],
                    op0=ALU.mult, op1=ALU.mult),
                    reads=[r_H(k), rn, "gains"], writes=[r_X(k)])
            S.add("sp", lambda e, ti=ti: e.dma_start(out=Hs[:, :, :].rearrange("p k t -> p (k t)"), in_=hscr[ti]),
                  reads=[("hscr", ti)], writes=[r_H(k) for k in range(KC)], chan="hld")
            xr = [r_X(k) for k in range(KC)]
            for m in range(KC):
                s = wload(wo[m], KC * 128)
                wview = Wr[s][:, 0:KC * 128].rearrange("p (k c) -> p k c", k=KC)
                b = gbank()
                mm_group(b, KC, lambda k, wview=wview: wview[:, k, :], lambda k: Xs[:, k, :], reads=[("W", s)] + xr)
                S.add("dve", lambda e, b=b, m=m: e.tensor_tensor(out=Hs[:, m, :], in0=PS[b][:, :], in1=Hs[:, m, :], op=ALU.add),
                      reads=[("PS", b), r_H(m)], writes=[r_H(m)])
            norm(3)
            ffn(1)
            norm(4, final=True)
            for tb in range(4):
                sl = tb % 2
                for kq in range(KC // 4):
                    b = gbank()

                    def emit(e, tb=tb, kq=kq, b=b):
                        ins = None
                        for j in range(4):
                            ins = e.transpose(PS[b][:, j * 128:(j + 1) * 128], Hs[:, kq * 4 + j, tb * 128:(tb + 1) * 128], ident[:, :])
                        return ins
                    S.add("pe", emit, reads=[r_H(kq * 4 + j) for j in range(4)] + ["ident"], writes=[("PS", b)])
                    rr = ares(sl * 2 * D + kq * 1024, 1024)
                    if kq % 2 == 0:
                        S.add("dve", lambda e, sl=sl, kq=kq, b=b: e.tensor_copy(out=XS[sl][:, kq * 512:(kq + 1) * 512], in_=PS[b][:, :]),
                              reads=[("PS", b)], writes=rr)
                    else:
                        S.add("act", lambda e, sl=sl, kq=kq, b=b: e.activation(out=XS[sl][:, kq * 512:(kq + 1) * 512], in_=PS[b][:, :], func=AF.Copy),
                              reads=[("PS", b)], writes=rr)
                r0 = ti * TT + tb * 128
                S.add("sp", lambda e, sl=sl, r0=r0: e.dma_start(out=y_d[r0:r0 + 128, :], in_=XS[sl]),
                      reads=r_xs(sl), writes=[("y", r0)], chan=("XS", sl))

        engs = ["pe", "act", "dve", "pool", "sp"]
        chans = sorted(S.chan_count.keys(), key=str)
        sem_e = {e: es.enter_context(nc.semaphore(f"e_{e}")) for e in ["pe", "act", "dve", "pool"] if S.eng_count.get(e)}
        sem_c = {c: es.enter_context(nc.semaphore(f"c{i}")) for i, c in enumerate(chans)}

        def run_engine(eng_name, e):
            waited = {}
            for op in S.ops:
                if op.eng != eng_name:
                    continue
                need = {}
                for d in op.deps:
                    if d.chan is None:
                        if d.eng == eng_name and eng_name == "pe":
                            continue
                        key = ("e", d.eng)
                        sem = sem_e[d.eng]
                    else:
                        key = ("c", d.chan)
                        sem = sem_c[d.chan]
                    if need.get(key, (None, 0))[1] < d.idx:
                        need[key] = (sem, d.idx)
                for key, (sem, val) in need.items():
                    if waited.get(key, 0) >= val:
                        continue
                    e.wait_ge(sem, val)
                    waited[key] = val
                ins = op.emit(e)
                if op.chan is None:
                    ins.then_inc(sem_e[op.eng], 1)
                else:
                    ins.then_inc(sem_c[op.chan], 16)
            if eng_name == "sp":
                for c in chans:
                    e.wait_ge(sem_c[c], S.chan_count[c])

        with nc.Block() as block:
            @block.tensor
            def _(e):
                run_engine("pe", e)

            @block.scalar
            def _(e):
                run_engine("act", e)

            @block.vector
            def _(e):
                run_engine("dve", e)

            @block.gpsimd
            def _(e):
                run_engine("pool", e)

            @block.sync
            def _(e):
                run_engine("sp", e)
    return nc


def _structure(cfg):
    seq_id = np.concatenate([np.full(L, i, np.int64) for i, L in enumerate(cfg.SEQS)])
    pos = np.concatenate([np.arange(L, dtype=np.int64) for L in cfg.SEQS])
    slen = np.concatenate([np.full(L, L, np.int64) for L in cfg.SEQS])
    return seq_id, pos, slen


def host_prepare(cfg, inp):
    D, KC, FC, FG, HA, HB, NH = cfg.D, cfg.KC, cfg.FC, cfg.FG, cfg.HA, cfg.HB, cfg.NH
    OWN, EXT, NTO, HT = cfg.OWN, cfg.EXT, cfg.NTO, cfg.HT
    WA, WB = HA * 128, HB * 128
    f32 = np.float32
    stream = np.concatenate([np.asarray(inp["x_prompt"], f32).reshape(-1, D), np.asarray(inp["x_sample"], f32).reshape(-1, D)], 0)
    NTOK = stream.shape[0]
    seq_id, pos, slen = _structure(cfg)

    def gu_layout(wg):
        a = np.ascontiguousarray(np.asarray(wg, f32).reshape(KC, 128, FC, 128).transpose(2, 1, 0, 3))
        return a.reshape(FC, 128, KC * 128)

    def d_layout(wdn):
        a = np.asarray(wdn, f32).reshape(2, FG, 128, KC, 128).transpose(0, 3, 2, 1, 4)
        return np.ascontiguousarray(a).reshape(2 * KC, 128, FG * 128)

    def cols_layout(w, col_starts):
        n = len(col_starts)
        a = np.empty((n, 128, KC, 128), f32)
        w = np.asarray(w, f32)
        for i, c0 in enumerate(col_starts):
            a[i] = w[:, c0:c0 + 128].reshape(KC, 128, 128).transpose(1, 0, 2)
        return a.reshape(n, 128, KC * 128)

    w_in = np.asarray(inp["w_in"], f32)[0]
    kcols = [WA + h * 128 for h in range(HA)] + [3 * WA + WB + h * 128 for h in range(HB)]
    qcols = [h * 128 for h in range(HA)] + [3 * WA + h * 128 for h in range(HB)]
    wqk = cols_layout(w_in, kcols + qcols)
    vcols = [2 * WA + h * 128 for h in range(HA)] + [3 * WA + 2 * WB + h * 128 for h in range(HB)]
    wv = cols_layout(w_in, vcols)
    wo = cols_layout(np.asarray(inp["w_out"], f32)[0], [m * 128 for m in range(KC)])

    def gl(v):
        return np.asarray(v, f32).reshape(KC, 128).T
    gains = np.concatenate([
        gl(inp["ffn1_norm"][0]), gl(inp["mix_norm"][0]),
        gl(np.concatenate([np.asarray(inp["out_norm_a"][0], f32), np.asarray(inp["out_norm_b"][0], f32)])),
        gl(inp["ffn2_norm"][0]), gl(inp["final_norm"])], 1)
    gains = np.ascontiguousarray(gains, f32)

    consts = np.zeros((128, 256), f32)
    consts[:, 0:128] = np.eye(128, dtype=f32)
    for d in range(16):
        consts[d + 16, 128 + d] = -1.0
        consts[d, 128 + d + 16] = 1.0

    p = np.arange(128)[:, None]
    f = np.arange(TT)[None, :]
    maskA = np.zeros((128, 20, TT), f32)
    for kb in range(20):
        dl = -1024 + kb * 128 + p - f
        ad = np.abs(dl)
        maskA[:, kb, :] = ((ad <= 64).astype(f32) + ((dl % 4 == 0) & (ad <= 256)).astype(f32)
                           + ((dl % 16 == 0) & (ad <= 1024)).astype(f32))
    maskA = maskA.reshape(128, 20 * TT)

    rel = np.asarray(inp["nbr_rel_bias"], f32)[0]
    gb = np.zeros((HB, 128, 8, TT), f32)
    for kb in range(8):
        jrel = -256 + kb * 128 + p
        drow = (jrel // 64) - (f // 64)
        dr = drow + 7
        dc = np.clip((jrel % 64) - (f % 64), -15, 15) + 15
        ok = (dr >= 0) & (dr <= 14)
        g = rel[:, np.clip(dr, 0, 14), dc]
        gb[:, :, kb, :] = np.where(ok[None], g, 0.0)
    gb = gb.reshape(HB, 128, 8 * TT)
    relb = np.ascontiguousarray(rel.reshape(HB, 465))

    inv = (ROPE_THETA ** (-np.arange(0, 32, 2, dtype=f32) / f32(32))).astype(f32)

    shared = dict(wg1=gu_layout(inp["ffn1_w_gate"][0]), wu1=gu_layout(inp["ffn1_w_up"][0]), wd1=d_layout(inp["ffn1_w_down"][0]),
                  wg2=gu_layout(inp["ffn2_w_gate"][0]), wu2=gu_layout(inp["ffn2_w_up"][0]), wd2=d_layout(inp["ffn2_w_down"][0]),
                  wqk=wqk, wv=wv, wo=wo, gains=gains, consts=consts, relb=relb, gb=gb, maskA=maskA)
    in_maps = []
    for c in range(cfg.NCORES):
        g0 = c * OWN - HALO
        gidx = np.arange(g0, g0 + EXT)
        valid = (gidx >= 0) & (gidx < NTOK)
        gcl = np.clip(gidx, 0, NTOK - 1)
        x_ext = np.where(valid[:, None], stream[gcl], f32(0.0)).astype(f32)
        e_seq = np.where(valid, seq_id[gcl], -1)
        e_pos = np.where(valid, pos[gcl], 0)
        e_len = np.where(valid, slen[gcl], 64 * 8)
        ang = (e_pos.astype(f32)[None, :] * inv[:, None]).astype(f32)
        ropeC = np.ones((128, EXT), f32); ropeS = np.zeros((128, EXT), f32)
        ropeC[0:16] = np.cos(ang); ropeC[16:32] = np.cos(ang)
        ropeS[0:16] = np.sin(ang); ropeS[16:32] = np.sin(ang)
        vflag = np.zeros((NTO, 20), f32)
        maskB = np.zeros((NTO, 128, 8, TT), f32)
        for ti in range(NTO):
            i0 = (ti + HT) * TT
            qs = e_seq[i0]
            for kb in range(20):
                j0 = i0 - 1024 + kb * 128
                vflag[ti, kb] = 1.0 if (e_seq[j0] == qs and qs >= 0) else 0.0
            qpos = e_pos[i0:i0 + TT]; rows = e_len[i0] // 64
            r = qpos // 64; cc = qpos % 64
            rs = np.clip(r - 4, 0, rows - 8); cs = np.clip(cc - 8, 0, 64 - 16)
            for kb in range(8):
                j0 = i0 - 256 + kb * 128
                kp = e_pos[j0:j0 + 128]; ks = e_seq[j0:j0 + 128]
                rho = kp // 64; gam = kp % 64
                ok = ((ks[:, None] == qs) & (qs >= 0) & (rho[:, None] >= rs[None, :]) & (rho[:, None] < rs[None, :] + 8)
                      & (gam[:, None] >= cs[None, :]) & (gam[:, None] < cs[None, :] + 16))
                maskB[ti, :, kb, :] = ok.astype(f32)
        m = dict(shared)
        m.update(x_ext=x_ext, ropeC=ropeC, ropeS=ropeS,
                 vflag=np.ascontiguousarray(np.broadcast_to(vflag.reshape(1, NTO * 20), (128, NTO * 20)), f32),
                 maskB=maskB.reshape(NTO, 128, 8 * TT))
        in_maps.append(m)
    return in_maps


def run(cfg, inp):
    in_maps = host_prepare(cfg, inp)
    nc = build_program(cfg)
    res = run_bass_kernel_spmd(nc, in_maps, core_ids=list(range(cfg.NCORES)))
    ys = np.concatenate([np.asarray(r["y"], np.float32) for r in res.results], 0)
    return ys


def kernel(x_prompt, x_sample, **w):
    cfg = Cfg()
    inp = dict(w)
    inp["x_prompt"] = x_prompt
    inp["x_sample"] = x_sample
    ys = run(cfg, inp)
    D = cfg.D
    npr = x_prompt.shape[0] * x_prompt.shape[1]
    y_prompt = ys[:npr].reshape(x_prompt.shape).astype(np.float32)
    y_sample = ys[npr:].reshape(x_sample.shape).astype(np.float32)
    return (y_prompt, y_sample)
```

```python
import numpy as np
import concourse.bass as bass
import concourse.mybir as mybir
from concourse.bass_utils import run_bass_kernel_spmd

F32 = mybir.dt.float32
BF16 = mybir.dt.bfloat16
AF = mybir.ActivationFunctionType
ALU = mybir.AluOpType
AX = mybir.AxisListType

HALO = 1024
TT = 512
ROPE_THETA = 500000.0
EPS = 1e-6


class Cfg:
    def __init__(self, D=4096, DFF=11008, HA=16, HB=16, NCORES=8, OWN=3072,
                 SEQS=(8192, 8192, 4096, 4096)):
        self.D, self.DFF, self.HA, self.HB = D, DFF, HA, HB
        self.NCORES, self.OWN, self.SEQS = NCORES, OWN, tuple(SEQS)
        self.KC = D // 128
        self.FC = DFF // 128
        self.FG = self.FC // 2
        self.NH = HA + HB
        self.EXT = OWN + 2 * HALO
        self.NTE = self.EXT // TT
        self.NTO = OWN // TT
        self.HT = HALO // TT
        assert self.FC % 2 == 0 and HA % 2 == 0 and HB % 2 == 0
        assert (HA + HB) * 128 == D and self.KC % 4 == 0
        assert sum(SEQS) == NCORES * OWN
        self.WSLOT = max(self.KC * 128, self.FG * 128)
        self.ABYTES = max(self.FG * TT * 2, 43008, 2 * D * 4)
        self.BIGW = self.WSLOT * 4 >= 28 * TT


class Op:
    __slots__ = ("eng", "emit", "deps", "chan", "idx")


class Sched:
    def __init__(self):
        self.ops = []
        self.lastw = {}
        self.readers = {}
        self.eng_count = {}
        self.chan_count = {}

    def add(self, eng, emit, reads=(), writes=(), chan=None):
        op = Op()
        op.eng, op.emit, op.chan = eng, emit, chan
        deps = set()
        for r in reads:
            w = self.lastw.get(r)
            if w is not None:
                deps.add(w)
        for w_ in writes:
            w = self.lastw.get(w_)
            if w is not None:
                deps.add(w)
            rs = self.readers.get(w_)
            if rs:
                deps.update(rs)
        op.deps = deps
        for r in reads:
            self.readers.setdefault(r, []).append(op)
        for w_ in writes:
            self.lastw[w_] = op
            self.readers[w_] = []
        if chan is None:
            self.eng_count[eng] = self.eng_count.get(eng, 0) + 1
            op.idx = self.eng_count[eng]
        else:
            self.chan_count[chan] = self.chan_count.get(chan, 0) + 16
            op.idx = self.chan_count[chan]
        self.ops.append(op)
        return op


def build_program(cfg):
    D, KC, FC, FG = cfg.D, cfg.KC, cfg.FC, cfg.FG
    HA, HB, NH = cfg.HA, cfg.HB, cfg.NH
    OWN, EXT, NTE, NTO, HT = cfg.OWN, cfg.EXT, cfg.NTE, cfg.NTO, cfg.HT
    NBLK = EXT // 128
    WSLOT = cfg.WSLOT
    NBW = 4
    SCALE = 128.0 ** -0.5

    nc = bass.Bass("TRN2", target_bir_lowering=False)

    def din(name, shape, dt=F32):
        return nc.dram_tensor(name, list(shape), dt, kind="ExternalInput").ap()

    x_ext = din("x_ext", [EXT, D])
    wg = [din("wg1", [FC, 128, KC * 128]), din("wg2", [FC, 128, KC * 128])]
    wu = [din("wu1", [FC, 128, KC * 128]), din("wu2", [FC, 128, KC * 128])]
    wd = [din("wd1", [2 * KC, 128, FG * 128]), din("wd2", [2 * KC, 128, FG * 128])]
    wqk = din("wqk", [2 * NH, 128, KC * 128])
    wv = din("wv", [NH, 128, KC * 128])
    wo = din("wo", [KC, 128, KC * 128])
    gains_d = din("gains", [128, 5 * KC])
    ropeC_d = din("ropeC", [128, EXT])
    ropeS_d = din("ropeS", [128, EXT])
    consts_d = din("consts", [128, 256])
    relb_d = din("relb", [HB, 465])
    gb_d = din("gb", [HB, 128, 8 * TT])
    maskA_d = din("maskA", [128, 20 * TT])
    vflag_d = din("vflag", [128, NTO * 20])
    maskB_d = din("maskB", [NTO, 128, 8 * TT])
    y_d = nc.dram_tensor("y", [OWN, D], F32, kind="ExternalOutput").ap()
    hscr = nc.dram_tensor("hscr", [NTO, 128, KC * TT], F32, kind="Internal").ap()
    qscr = nc.dram_tensor("qscr", [NH, 128, OWN], BF16, kind="Internal").ap()
    kscr = nc.dram_tensor("kscr", [NH, 128, EXT], BF16, kind="Internal").ap()
    vscr = nc.dram_tensor("vscr", [NH, 128, NBLK * 128], BF16, kind="Internal").ap()

    S = Sched()
    import contextlib
    es = contextlib.ExitStack()
    with es:
        def sb(name, shape, dt):
            return es.enter_context(nc.sbuf_tensor("s_" + name, list(shape), dt))

        Hs = sb("H", [128, KC, TT], F32)
        Xs = sb("X", [128, KC, TT], BF16)
        Ar = sb("A", [128, cfg.ABYTES // 2], BF16)
        Wall = sb("Wall", [128, NBW * WSLOT], BF16)
        Wr = [Wall[:, i * WSLOT:(i + 1) * WSLOT] for i in range(NBW)]
        ident = sb("ident", [128, 128], F32)
        piT = sb("piT", [128, 128], F32)
        identb = sb("identb", [128, 128], BF16)
        onesb = sb("onesb", [128, 128], BF16)
        gains = sb("gains", [128, 5 * KC], F32)
        RS = sb("RS", [128, TT], F32)
        SQ = [sb(f"SQ{i}", [128, TT], BF16) for i in range(2)]
        TS = [sb(f"TS{i}", [128, TT], BF16) for i in range(2)]
        SQ2 = [sb(f"SQ2{i}", [128, TT], BF16) for i in range(2)]
        CT = sb("CT", [128, TT], F32)
        ST = sb("ST", [128, TT], F32)
        QF = [sb(f"QF{i}", [128, TT], F32) for i in range(2)]
        T1 = sb("T1", [128, TT], F32)
        T2 = sb("T2", [128, TT], F32)
        RS2 = T2
        OUTB = TS
        VO = [sb(f"VO{i}", [128, TT], BF16) for i in range(2)]
        KMAX = sb("KMAX", [128, NH], F32)
        QMAX = sb("QMAX", [128, NH], F32)
        NEGM = sb("NEGM", [128, NH], F32)
        BSS = sb("BSS", [128, HB], F32)
        TM = [sb(f"TM{i}", [128, 1], F32) for i in range(2)]
        VFL = sb("VFL", [128, NTO * 20], F32)
        RD = T1
        EB2 = VO[0]
        PB2 = VO[1]
        PS = [es.enter_context(nc.psum_tensor(f"ps{i}", [128, TT], F32)) for i in range(8)]

        A_act = Ar[:, 0:FG * TT].rearrange("p (f t) -> p f t", t=TT)
        XS = [Ar[:, i * 2 * D:(i + 1) * 2 * D].bitcast(F32) for i in range(2)]
        o = 0
        QT = []; KT = []; VB = []; GB = []; EB = []; PB = []
        for i in range(2):
            QT.append(Ar[:, o:o + TT]); o += TT
        for i in range(2):
            KT.append(Ar[:, o:o + 2560]); o += 2560
        for i in range(2):
            VB.append(Ar[:, o:o + 2560].rearrange("p (b d) -> p b d", d=128)); o += 2560
        for i in range(2):
            GB.append(Ar[:, o:o + 8 * TT].rearrange("p (b t) -> p b t", t=TT)); o += 8 * TT
        for i in range(2):
            EB.append(Ar[:, o:o + TT]); o += TT
        for i in range(2):
            PB.append(Ar[:, o:o + TT]); o += TT
        assert o * 2 <= cfg.ABYTES
        A_RES = 1024

        def ares(lo_el, n_el):
            lo = lo_el * 2; hi = (lo_el + n_el) * 2
            return [("A", i) for i in range(lo // A_RES, (hi + A_RES - 1) // A_RES)]

        r_act = lambda f: ares(f * TT, TT)
        r_xs = lambda i: ares(i * 2 * D, 2 * D)
        _o = [0]

        def _nx(n):
            r = ares(_o[0], n); _o[0] += n; return r
        r_QT = [_nx(TT) for i in range(2)]
        r_KT = [_nx(2560) for i in range(2)]
        r_VB = [_nx(2560) for i in range(2)]
        r_GB = [_nx(8 * TT) for i in range(2)]
        r_EB = [_nx(TT) for i in range(2)]
        r_PB = [_nx(TT) for i in range(2)]
        if cfg.BIGW:
            MA = lambda kb: Wall[:, kb * TT:(kb + 1) * TT]
            MBv = lambda kb: Wall[:, (20 + kb) * TT:(21 + kb) * TT]
            ma_dst = [(Wall[:, 0:20 * TT], 0, 20)]
            mb_dst = Wall[:, 20 * TT:28 * TT]
            r_mask = [("W", i) for i in range((28 * TT + WSLOT - 1) // WSLOT)]
        else:
            MAt = sb("MAt", [128, 20 * TT], BF16)
            MBt = sb("MBt", [128, 8 * TT], BF16)
            MA = lambda kb: MAt[:, kb * TT:(kb + 1) * TT]
            MBv = lambda kb: MBt[:, kb * TT:(kb + 1) * TT]
            ma_dst = [(MAt[:, :], 0, 20)]
            mb_dst = MBt[:, :]
            r_mask = [("MASK",)]

        bank_rr = [0]

        def gbank():
            b = bank_rr[0] % 5
            bank_rr[0] += 1
            return b

        wslot_rr = [0]

        def wload(src_ap, nel):
            s = wslot_rr[0] % NBW
            wslot_rr[0] += 1
            S.add("pool", lambda e, s=s, src_ap=src_ap, nel=nel: e.dma_start(out=Wr[s][:, 0:nel], in_=src_ap),
                  writes=[("W", s)], chan=("W", s))
            return s

        pe_defer = []

        def defer(fn, delay=0):
            pe_defer.append([delay, fn])

        def flush_defer():
            cur = list(pe_defer)
            del pe_defer[:]
            for ent in cur:
                if ent[0] <= 0:
                    ent[1]()
                else:
                    ent[0] -= 1
                    pe_defer.append(ent)

        def drain_defer():
            while pe_defer:
                flush_defer()

        sq_pending = []

        def sq_accum(k, nchunks=None):
            n = KC if nchunks is None else nchunks
            if (k % 2) in sq_pending:
                flush_defer()
            assert (k % 2) not in sq_pending
            S.add("act", lambda e, k=k: e.activation(out=SQ[k % 2][:, :], in_=Hs[:, k, :], func=AF.Square),
                  reads=[r_H(k)], writes=[("SQ", k % 2)])
            sq_pending.append(k % 2)

            def pe_part(k=k):
                sq_pending.remove(k % 2)
                S.add("pe", lambda e: e.matmul(PS[6][:, :], lhsT=onesb[:, :], rhs=SQ[k % 2][:, :],
                                               start=(k == 0), stop=(k == n - 1)),
                      reads=[("SQ", k % 2), "onesb"], writes=[("PS", 6)])
            defer(pe_part)

        def mm_group(bank, nmm, lhs_fn, rhs_fn, reads, out_ap=None):
            outp = PS[bank][:, :] if out_ap is None else out_ap

            def emit(e):
                ins = None
                for k in range(nmm):
                    ins = e.matmul(outp, lhsT=lhs_fn(k), rhs=rhs_fn(k), start=(k == 0), stop=(k == nmm - 1))
                return ins
            S.add("pe", emit, reads=reads, writes=[("PS", bank)])

        S.add("sp", lambda e: e.dma_start(out=ident[:, :], in_=consts_d[:, 0:128]), writes=["ident"], chan="c_id")
        S.add("sp", lambda e: e.dma_start(out=piT[:, :], in_=consts_d[:, 128:256]), writes=["piT"], chan="c_pi")
        S.add("sp", lambda e: e.dma_start(out=gains[:, :], in_=gains_d[:, :]), writes=["gains"], chan="c_g")
        S.add("sp", lambda e: e.dma_start(out=VFL[:, :], in_=vflag_d[:, :]), writes=["VFL"], chan="c_vf")
        S.add("dve", lambda e: e.memset(onesb[:, :], 1.0), writes=["onesb"])
        S.add("dve", lambda e: e.tensor_copy(out=identb[:, :], in_=ident[:, :]), reads=["ident"], writes=["identb"])
        S.add("dve", lambda e: e.memset(KMAX[:, :], 0.0), writes=["KMAX"])
        S.add("dve", lambda e: e.memset(QMAX[:, :], 0.0), writes=["QMAX"])
        S.add("dve", lambda e: e.memset(BSS[:, :], 0.0), writes=["BSS"])

        r_H = lambda k: ("H", k)
        r_X = lambda k: ("X", k)

        def norm(gidx, final=False):
            drain_defer()
            S.add("dve", lambda e: e.tensor_scalar(out=RS[:, :], in0=PS[6][:, :], scalar1=1.0 / D, scalar2=EPS,
                                                   op0=ALU.mult, op1=ALU.add),
                  reads=[("PS", 6)], writes=["RS"])
            S.add("act", lambda e: e.activation(out=RS[:, :], in_=RS[:, :], func=AF.Sqrt), reads=["RS"], writes=["RS"])
            S.add("dve", lambda e: e.reciprocal(out=RS[:, :], in_=RS[:, :]), reads=["RS"], writes=["RS"])
            for k in range(KC):
                if final:
                    S.add("dve", lambda e, k=k: e.scalar_tensor_tensor(
                        out=Hs[:, k, :], in0=Hs[:, k, :], scalar=gains[:, gidx * KC + k:gidx * KC + k + 1],
                        in1=RS[:, :], op0=ALU.mult, op1=ALU.mult),
                        reads=[r_H(k), "RS", "gains"], writes=[r_H(k)])
                else:
                    S.add("dve", lambda e, k=k: e.scalar_tensor_tensor(
                        out=Xs[:, k, :], in0=Hs[:, k, :], scalar=gains[:, gidx * KC + k:gidx * KC + k + 1],
                        in1=RS[:, :], op0=ALU.mult, op1=ALU.mult),
                        reads=[r_H(k), "RS", "gains"], writes=[r_X(k)])

        def ffn(li):
            xr = [r_X(k) for k in range(KC)]
            for g in range(2):
                for fl in range(FG):
                    f = g * FG + fl
                    s = wload(wg[li][f], KC * 128)
                    wg_ = Wr[s][:, 0:KC * 128].rearrange("p (k c) -> p k c", k=KC)
                    s2 = wload(wu[li][f], KC * 128)
                    wu_ = Wr[s2][:, 0:KC * 128].rearrange("p (k c) -> p k c", k=KC)
                    bg = gbank(); bu = gbank()
                    mm_group(bg, KC, lambda k, wg_=wg_: wg_[:, k, :], lambda k: Xs[:, k, :], reads=[("W", s)] + xr)
                    mm_group(bu, KC, lambda k, wu_=wu_: wu_[:, k, :], lambda k: Xs[:, k, :], reads=[("W", s2)] + xr)
                    flush_defer()
                    S.add("act", lambda e, bg=bg, fl=fl: e.activation(out=TS[fl % 2][:, :], in_=PS[bg][:, :], func=AF.Silu),
                          reads=[("PS", bg)], writes=[("TS", fl % 2)])
                    S.add("dve", lambda e, bu=bu, fl=fl: e.tensor_tensor(out=A_act[:, fl, :], in0=TS[fl % 2][:, :],
                                                                       in1=PS[bu][:, :], op=ALU.mult),
                          reads=[("PS", bu), ("TS", fl % 2)], writes=r_act(fl))
                ar = [r for fl in range(FG) for r in r_act(fl)]
                for m in range(KC):
                    s = wload(wd[li][g * KC + m], FG * 128)
                    wv_ = Wr[s][:, 0:FG * 128].rearrange("p (k c) -> p k c", k=FG)
                    b = gbank()
                    mm_group(b, FG, lambda k, wv_=wv_: wv_[:, k, :], lambda k: A_act[:, k, :], reads=[("W", s)] + ar)
                    flush_defer()
                    S.add("dve", lambda e, b=b, m=m: e.scalar_tensor_tensor(
                        out=Hs[:, m, :], in0=PS[b][:, :], scalar=0.5, in1=Hs[:, m, :], op0=ALU.mult, op1=ALU.add),
                        reads=[("PS", b), r_H(m)], writes=[r_H(m)])
                    if g == 1:
                        sq_accum(m)

        def qk_chunk(t, idx, h, is_q, is_a, wview, s, j):
            xr = [r_X(k) for k in range(KC)]
            b = gbank()
            mm_group(b, KC, lambda k: wview[:, k, :], lambda k: Xs[:, k, :], reads=[("W", s)] + xr)
            flush_defer()
            ob = idx % 2
            sc = SCALE if is_q else 1.0
            if is_a:
                S.add("act", lambda e: e.activation(out=QF[ob][:, :], in_=PS[b][:, :], func=AF.Copy, scale=sc),
                      reads=[("PS", b)], writes=[("QF", ob)])

                def d1():
                    S.add("pe", lambda e: e.matmul(PS[7][:, :], lhsT=piT[:, :], rhs=QF[ob][:, :], start=True, stop=True),
                          reads=[("QF", ob), "piT"], writes=[("PS", 7)])
                    S.add("dve", lambda e: e.tensor_tensor(out=T1[:, :], in0=QF[ob][:, :], in1=CT[:, :], op=ALU.mult),
                          reads=[("QF", ob), "CT"], writes=["T1"])
                    S.add("dve", lambda e: e.tensor_tensor(out=T2[:, :], in0=PS[7][:, :], in1=ST[:, :], op=ALU.mult),
                          reads=[("PS", 7), "ST"], writes=["T2"])
                    S.add("dve", lambda e: e.tensor_tensor(out=OUTB[ob][:, :], in0=T1[:, :], in1=T2[:, :], op=ALU.add),
                          reads=["T1", "T2"], writes=[("TS", ob)])
                    tail()
                defer(d1)
            else:
                S.add("act", lambda e: e.activation(out=OUTB[ob][:, :], in_=PS[b][:, :], func=AF.Copy, scale=sc),
                      reads=[("PS", b)], writes=[("TS", ob)])
                defer(lambda: tail())

            def tail():
                S.add("act", lambda e: e.activation(out=SQ2[ob][:, :], in_=OUTB[ob][:, :], func=AF.Square),
                      reads=[("TS", ob)], writes=[("SQ2", ob)])
                defer(tail2)

            def tail2():
                S.add("pe", lambda e: e.matmul(PS[5][:, :], lhsT=onesb[:, :], rhs=SQ2[ob][:, :], start=True, stop=True),
                      reads=[("SQ2", ob), "onesb"], writes=[("PS", 5)])
                S.add("dve", lambda e: e.tensor_reduce(out=TM[ob][:, :], in_=PS[5][:, :], axis=AX.X, op=ALU.max),
                      reads=[("PS", 5)], writes=[("TM", ob)])
                MX = QMAX if is_q else KMAX
                mxn = "QMAX" if is_q else "KMAX"
                S.add("dve", lambda e: e.tensor_tensor(out=MX[:, h:h + 1], in0=MX[:, h:h + 1], in1=TM[ob][:, :], op=ALU.max),
                      reads=[("TM", ob), mxn], writes=[mxn])
                if is_q:
                    to = t - HT
                    dst = qscr[h, :, to * TT:(to + 1) * TT]
                else:
                    dst = kscr[h, :, t * TT:(t + 1) * TT]
                S.add("sp", lambda e: e.dma_start(out=dst, in_=OUTB[ob][:, :]), reads=[("TS", ob)],
                      writes=[("scr", "q" if is_q else "k", h, t)], chan=("TS", ob))

        def xload_tb(t, tb):
            sl = tb % 2
            r0 = t * TT + tb * 128
            S.add("sp", lambda e: e.dma_start(out=XS[sl], in_=x_ext[r0:r0 + 128, :]), writes=r_xs(sl), chan=("XS", sl))

        def xtrans_tb(t, tb):
            sl = tb % 2
            for kq in range(KC // 4):
                b = gbank()

                def emit(e, kq=kq, b=b):
                    ins = None
                    for j in range(4):
                        ins = e.transpose(PS[b][:, j * 128:(j + 1) * 128], XS[sl][:, (kq * 4 + j) * 128:(kq * 4 + j + 1) * 128], ident[:, :])
                    return ins
                S.add("pe", emit, reads=r_xs(sl) + ["ident"], writes=[("PS", b)])
                flush_defer()
                outv = Hs[:, kq * 4:(kq + 1) * 4, tb * 128:(tb + 1) * 128]
                inv = PS[b][:, :].rearrange("p (a c) -> p a c", a=4)
                if kq % 2 == 0:
                    S.add("dve", lambda e, outv=outv, inv=inv: e.tensor_copy(out=outv, in_=inv),
                          reads=[("PS", b)], writes=[r_H(kq * 4 + j) for j in range(4)])
                else:
                    S.add("act", lambda e, outv=outv, inv=inv: e.activation(out=outv, in_=inv, func=AF.Copy),
                          reads=[("PS", b)], writes=[r_H(kq * 4 + j) for j in range(4)])
                if tb == 3:
                    for j in range(4):
                        sq_accum(kq * 4 + j)

        def xprep_all(t):
            xload_tb(t, 0); xload_tb(t, 1)
            for tb in range(4):
                xtrans_tb(t, tb)
                if tb + 2 < 4:
                    xload_tb(t, tb + 2)

        xprep_all(0)
        for t in range(NTE):
            own = HT <= t < HT + NTO
            S.add("sp", lambda e, t=t: e.dma_start(out=CT[:, :], in_=ropeC_d[:, t * TT:(t + 1) * TT]), writes=["CT"], chan="ropeC")
            S.add("sp", lambda e, t=t: e.dma_start(out=ST[:, :], in_=ropeS_d[:, t * TT:(t + 1) * TT]), writes=["ST"], chan="ropeS")
            norm(0)
            ffn(0)
            if t + 1 < NTE:
                xload_tb(t + 1, 0); xload_tb(t + 1, 1)
            norm(1)
            if own:
                S.add("sp", lambda e, t=t: e.dma_start(out=hscr[t - HT], in_=Hs[:, :, :].rearrange("p k t -> p (k t)")),
                      reads=[r_H(k) for k in range(KC)], writes=[("hscr", t - HT)], chan="hst")
            idx = 0
            nxt_tb = [0]
            far = (t == 0) or (t == NTE - 1)
            for h in range(NH):
                if far and h >= HA:
                    continue
                s = wload(wqk[h], KC * 128)
                wview = Wr[s][:, 0:KC * 128].rearrange("p (k c) -> p k c", k=KC)
                qk_chunk(t, idx, h, False, h < HA, wview, s, 0); idx += 1
                if t + 1 < NTE and idx % 4 == 0 and nxt_tb[0] < 4:
                    xtrans_tb(t + 1, nxt_tb[0])
                    if nxt_tb[0] + 2 < 4:
                        xload_tb(t + 1, nxt_tb[0] + 2)
                    nxt_tb[0] += 1
            while t + 1 < NTE and nxt_tb[0] < 4:
                xtrans_tb(t + 1, nxt_tb[0])
                if nxt_tb[0] + 2 < 4:
                    xload_tb(t + 1, nxt_tb[0] + 2)
                nxt_tb[0] += 1
            xr = [r_X(k) for k in range(KC)]
            vcnt = 0
            for h in range(NH):
                if far and h >= HA:
                    continue
                s = wload(wv[h], KC * 128)
                wview = Wr[s][:, 0:KC * 128].rearrange("p (k c) -> p k c", k=KC)
                b = gbank()
                for tb in range(4):
                    mm_group(b, KC, lambda k, tb=tb: Xs[:, k, tb * 128:(tb + 1) * 128], lambda k, wview=wview: wview[:, k, :],
                             reads=[("W", s)] + xr, out_ap=PS[b][:, tb * 128:(tb + 1) * 128])
                flush_defer()
                vb = vcnt % 2; vcnt += 1
                S.add("act", lambda e, b=b, vb=vb: e.activation(out=VO[vb][:, :], in_=PS[b][:, :], func=AF.Copy),
                      reads=[("PS", b)], writes=[("VO", vb)])
                S.add("sp", lambda e, h=h, vb=vb, t=t: e.dma_start(out=vscr[h, :, t * TT:(t + 1) * TT], in_=VO[vb][:, :]),
                      reads=[("VO", vb)], writes=[("scr", "v", h, t)], chan=("VO", vb))
            if own:
                for h in range(NH):
                    s = wload(wqk[NH + h], KC * 128)
                    wview = Wr[s][:, 0:KC * 128].rearrange("p (k c) -> p k c", k=KC)
                    qk_chunk(t, idx, h, True, h < HA, wview, s, 0); idx += 1
            flush_defer()

        drain_defer()
        for hb in range(HB):
            S.add("sp", lambda e, hb=hb: e.dma_start(out=T1[:, 0:465], in_=relb_d[hb:hb + 1, :].to_broadcast([128, 465])),
                  writes=["T1"], chan="relb")
            S.add("act", lambda e, hb=hb: e.activation(out=T2[:, 0:465], in_=T1[:, 0:465], func=AF.Square,
                                                       accum_out=BSS[:, hb:hb + 1]),
                  reads=["T1", "BSS"], writes=["T2", "BSS"])
        S.add("dve", lambda e: e.tensor_tensor(out=NEGM[:, :], in0=QMAX[:, :], in1=KMAX[:, :], op=ALU.mult),
              reads=["QMAX", "KMAX"], writes=["NEGM"])
        S.add("act", lambda e: e.activation(out=NEGM[:, :], in_=NEGM[:, :], func=AF.Sqrt), reads=["NEGM"], writes=["NEGM"])
        S.add("act", lambda e: e.activation(out=BSS[:, :], in_=BSS[:, :], func=AF.Sqrt), reads=["BSS"], writes=["BSS"])
        S.add("dve", lambda e: e.tensor_scalar(out=NEGM[:, :], in0=NEGM[:, :], scalar1=-1.02, scalar2=None, op0=ALU.mult),
              reads=["NEGM"], writes=["NEGM"])
        S.add("dve", lambda e: e.tensor_tensor(out=NEGM[:, HA:NH], in0=NEGM[:, HA:NH], in1=BSS[:, :], op=ALU.subtract),
              reads=["NEGM", "BSS"], writes=["NEGM"])

        def attn_loads(ti, h):
            par = h % 2
            i0 = (ti + HT) * TT
            is_a = h < HA
            klo = i0 - 1024 if is_a else i0 - 256
            nk = 2560 if is_a else 1024
            S.add("sp", lambda e: e.dma_start(out=QT[par], in_=qscr[h, :, ti * TT:(ti + 1) * TT]),
                  reads=[("scr", "q", h, ti + HT)], writes=r_QT[par], chan=("QT", par))
            S.add("sp", lambda e: e.dma_start(out=KT[par][:, 0:nk], in_=kscr[h, :, klo:klo + nk]),
                  reads=[("scr", "k", h, tt) for tt in range(klo // TT, (klo + nk - 1) // TT + 1)], writes=r_KT[par], chan=("KT", par))
            b0 = klo // 128
            nb = nk // 128
            S.add("sp", lambda e: e.dma_start(out=VB[par][:, 0:nb, :].rearrange("p b d -> p (b d)"), in_=vscr[h, :, b0 * 128:(b0 + nb) * 128]),
                  reads=[("scr", "v", h, tt) for tt in range(klo // TT, (klo + nk - 1) // TT + 1)], writes=r_VB[par], chan=("VB", par))
            if not is_a:
                S.add("pool", lambda e: e.dma_start(out=GB[par].rearrange("p b t -> p (b t)"), in_=gb_d[h - HA]),
                      writes=r_GB[par], chan=("GB", par))

        for ti in range(NTO):
            for (dst, k0, n) in ma_dst:
                S.add("pool", lambda e, dst=dst, k0=k0, n=n: e.dma_start(out=dst, in_=maskA_d[:, k0 * TT:(k0 + n) * TT]),
                      writes=r_mask, chan="MA")
            S.add("pool", lambda e, ti=ti: e.dma_start(out=mb_dst, in_=maskB_d[ti]), writes=r_mask, chan="MA")
            attn_loads(ti, 0)
            EBL = [EB[0], EB[1], EB2[:, :]]
            PBL = [PB[0], PB[1], PB2[:, :]]
            r_EBL = [r_EB[0], r_EB[1], [("VO", 0)]]
            r_PBL = [r_PB[0], r_PB[1], [("VO", 1)]]
            DEPTH = 3

            def rs_group(rs_t, rn, wdt):
                S.add("dve", lambda e: e.tensor_scalar(out=rs_t[:, :], in0=PS[7][:, :], scalar1=1.0 / wdt, scalar2=EPS,
                                                       op0=ALU.mult, op1=ALU.add), reads=[("PS", 7)], writes=[rn])
                S.add("act", lambda e: e.activation(out=rs_t[:, :], in_=rs_t[:, :], func=AF.Sqrt), reads=[rn], writes=[rn])
                S.add("dve", lambda e: e.reciprocal(out=rs_t[:, :], in_=rs_t[:, :]), reads=[rn], writes=[rn])

            for h in range(NH):
                if h + 1 < NH:
                    attn_loads(ti, h + 1)
                par = h % 2
                is_a = h < HA
                nkb = 20 if is_a else 8
                bO = 3 + par; bD = 5 + par

                def s_op(kb, par=par, is_a=is_a, h=h):
                    bS = kb % DEPTH
                    if is_a:
                        S.add("pe", lambda e: e.matmul(PS[bS][:, :], lhsT=KT[par][:, kb * 128:(kb + 1) * 128], rhs=QT[par], start=True, stop=True),
                              reads=r_QT[par] + r_KT[par], writes=[("PS", bS)])
                    else:
                        def emit(e):
                            e.matmul(PS[bS][:, :], lhsT=KT[par][:, kb * 128:(kb + 1) * 128], rhs=QT[par], start=True, stop=False)
                            return e.matmul(PS[bS][:, :], lhsT=identb[:, :], rhs=GB[par][:, kb, :], start=False, stop=True)
                        S.add("pe", emit, reads=r_QT[par] + r_KT[par] + r_GB[par] + ["identb"], writes=[("PS", bS)])
                for kb in range(DEPTH):
                    s_op(kb)
                flush_defer()
                for kb in range(nkb):
                    bS = kb % DEPTH
                    eb = kb % DEPTH
                    S.add("act", lambda e, eb=eb, bS=bS, h=h: e.activation(out=EBL[eb], in_=PS[bS][:, :], func=AF.Exp,
                                                                         bias=NEGM[:, h:h + 1], scale=1.0),
                          reads=[("PS", bS), "NEGM"], writes=r_EBL[eb])
                    if is_a:
                        S.add("dve", lambda e, kb=kb, eb=eb, ti=ti: e.scalar_tensor_tensor(
                            out=PBL[eb], in0=EBL[eb], scalar=VFL[:, ti * 20 + kb:ti * 20 + kb + 1], in1=MA(kb),
                            op0=ALU.mult, op1=ALU.mult),
                            reads=r_EBL[eb] + r_mask + ["VFL"], writes=r_PBL[eb])
                    else:
                        S.add("dve", lambda e, kb=kb, eb=eb: e.tensor_tensor(out=PBL[eb], in0=EBL[eb], in1=MBv(kb), op=ALU.mult),
                              reads=r_EBL[eb] + r_mask, writes=r_PBL[eb])

                    def emit(e, kb=kb, eb=eb, par=par, bO=bO, bD=bD, nkb=nkb):
                        e.matmul(PS[bO][:, :], lhsT=VB[par][:, kb, :], rhs=PBL[eb], start=(kb == 0), stop=(kb == nkb - 1))
                        return e.matmul(PS[bD][:, :], lhsT=onesb[:, :], rhs=PBL[eb], start=(kb == 0), stop=(kb == nkb - 1))
                    S.add("pe", emit, reads=r_PBL[eb] + r_VB[par] + ["onesb"], writes=[("PS", bO), ("PS", bD)])
                    if kb + DEPTH < nkb:
                        s_op(kb + DEPTH)
                    if kb == 3:
                        flush_defer()
                        if h == HA:
                            rs_group(RS, "RS", HA * 128)
                S.add("dve", lambda e, bD=bD: e.reciprocal(out=RD[:, :], in_=PS[bD][:, :]), reads=[("PS", bD)], writes=["T1"])
                S.add("dve", lambda e, bO=bO, h=h: e.tensor_tensor(out=Hs[:, h, :], in0=PS[bO][:, :], in1=RD[:, :], op=ALU.mult),
                      reads=[("PS", bO), "T1"], writes=[r_H(h)])
                S.add("act", lambda e, h=h: e.activation(out=SQ[h % 2][:, :], in_=Hs[:, h, :], func=AF.Square),
                      reads=[r_H(h)], writes=[("SQ", h % 2)])
                first = (h == 0) or (h == HA)
                last = (h == HA - 1) or (h == NH - 1)

                def dss(h=h, first=first, last=last):
                    S.add("pe", lambda e: e.matmul(PS[7][:, :], lhsT=onesb[:, :], rhs=SQ[h % 2][:, :], start=first, stop=last),
                          reads=[("SQ", h % 2), "onesb"], writes=[("PS", 7)])
                defer(dss, delay=1)
            drain_defer()
            rs_group(RS2, "T2", HB * 128)
            for k in range(KC):
                rs_t = RS if k < HA else RS2
                rn = "RS" if k < HA else "T2"
                S.add("dve", lambda e, k=k, rs_t=rs_t: e.scalar_tensor_tensor(
                    out=Xs[:, k, :], in0=Hs[:, k, :], scalar=gains[:, 2 * KC + k:2 * KC + k + 1], in1=rs_t[:, :],
                    op0=ALU.mult, op1=ALU.mult),
                    reads=[r_H(k), rn, "gains"], writes=[r_X(k)])
            S.add("sp", lambda e, ti=ti: e.dma_start(out=Hs[:, :, :].rearrange("p k t -> p (k t)"), in_=hscr[ti]),
                  reads=[("hscr", ti)], writes=[r_H(k) for k in range(KC)], chan="hld")
            xr = [r_X(k) for k in range(KC)]
            for m in range(KC):
                s = wload(wo[m], KC * 128)
                wview = Wr[s][:, 0:KC * 128].rearrange("p (k c) -> p k c", k=KC)
                b = gbank()
                mm_group(b, KC, lambda k, wview=wview: wview[:, k, :], lambda k: Xs[:, k, :], reads=[("W", s)] + xr)
                flush_defer()
                S.add("dve", lambda e, b=b, m=m: e.tensor_tensor(out=Hs[:, m, :], in0=PS[b][:, :], in1=Hs[:, m, :], op=ALU.add),
                      reads=[("PS", b), r_H(m)], writes=[r_H(m)])
                sq_accum(m)
            norm(3)
            ffn(1)
            norm(4, final=True)
            for tb in range(4):
                sl = tb % 2
                for kq in range(KC // 4):
                    b = gbank()

                    def emit(e, tb=tb, kq=kq, b=b):
                        ins = None
                        for j in range(4):
                            ins = e.transpose(PS[b][:, j * 128:(j + 1) * 128], Hs[:, kq * 4 + j, tb * 128:(tb + 1) * 128], ident[:, :])
                        return ins
                    S.add("pe", emit, reads=[r_H(kq * 4 + j) for j in range(4)] + ["ident"], writes=[("PS", b)])
                    rr = ares(sl * 2 * D + kq * 1024, 1024)
                    if kq % 2 == 0:
                        S.add("dve", lambda e, sl=sl, kq=kq, b=b: e.tensor_copy(out=XS[sl][:, kq * 512:(kq + 1) * 512], in_=PS[b][:, :]),
                              reads=[("PS", b)], writes=rr)
                    else:
                        S.add("act", lambda e, sl=sl, kq=kq, b=b: e.activation(out=XS[sl][:, kq * 512:(kq + 1) * 512], in_=PS[b][:, :], func=AF.Copy),
                              reads=[("PS", b)], writes=rr)
                r0 = ti * TT + tb * 128
                S.add("sp", lambda e, sl=sl, r0=r0: e.dma_start(out=y_d[r0:r0 + 128, :], in_=XS[sl]),
                      reads=r_xs(sl), writes=[("y", r0)], chan=("XS", sl))

        engs = ["pe", "act", "dve", "pool", "sp"]
        chans = sorted(S.chan_count.keys(), key=str)
        sem_e = {e: es.enter_context(nc.semaphore(f"e_{e}")) for e in ["pe", "act", "dve", "pool"] if S.eng_count.get(e)}
        sem_c = {c: es.enter_context(nc.semaphore(f"c{i}")) for i, c in enumerate(chans)}

        def run_engine(eng_name, e):
            waited = {}
            for op in S.ops:
                if op.eng != eng_name:
                    continue
                need = {}
                for d in op.deps:
                    if d.chan is None:
                        if d.eng == eng_name and eng_name == "pe":
                            continue
                        key = ("e", d.eng)
                        sem = sem_e[d.eng]
                    else:
                        key = ("c", d.chan)
                        sem = sem_c[d.chan]
                    if need.get(key, (None, 0))[1] < d.idx:
                        need[key] = (sem, d.idx)
                for key, (sem, val) in need.items():
                    if waited.get(key, 0) >= val:
                        continue
                    e.wait_ge(sem, val)
                    waited[key] = val
                ins = op.emit(e)
                if op.chan is None:
                    ins.then_inc(sem_e[op.eng], 1)
                else:
                    ins.then_inc(sem_c[op.chan], 16)
            if eng_name == "sp":
                for c in chans:
                    e.wait_ge(sem_c[c], S.chan_count[c])

        with nc.Block() as block:
            @block.tensor
            def _(e):
                run_engine("pe", e)

            @block.scalar
            def _(e):
                run_engine("act", e)

            @block.vector
            def _(e):
                run_engine("dve", e)

            @block.gpsimd
            def _(e):
                run_engine("pool", e)

            @block.sync
            def _(e):
                run_engine("sp", e)
    return nc


def _structure(cfg):
    seq_id = np.concatenate([np.full(L, i, np.int64) for i, L in enumerate(cfg.SEQS)])
    pos = np.concatenate([np.arange(L, dtype=np.int64) for L in cfg.SEQS])
    slen = np.concatenate([np.full(L, L, np.int64) for L in cfg.SEQS])
    return seq_id, pos, slen


def host_prepare(cfg, inp):
    D, KC, FC, FG, HA, HB, NH = cfg.D, cfg.KC, cfg.FC, cfg.FG, cfg.HA, cfg.HB, cfg.NH
    OWN, EXT, NTO, HT = cfg.OWN, cfg.EXT, cfg.NTO, cfg.HT
    WA, WB = HA * 128, HB * 128
    f32 = np.float32
    stream = np.concatenate([np.asarray(inp["x_prompt"], f32).reshape(-1, D), np.asarray(inp["x_sample"], f32).reshape(-1, D)], 0)
    NTOK = stream.shape[0]
    seq_id, pos, slen = _structure(cfg)

    def gu_layout(wg):
        a = np.ascontiguousarray(np.asarray(wg, f32).reshape(KC, 128, FC, 128).transpose(2, 1, 0, 3))
        return a.reshape(FC, 128, KC * 128)

    def d_layout(wdn):
        a = np.asarray(wdn, f32).reshape(2, FG, 128, KC, 128).transpose(0, 3, 2, 1, 4)
        return np.ascontiguousarray(a).reshape(2 * KC, 128, FG * 128)

    def cols_layout(w, col_starts):
        n = len(col_starts)
        a = np.empty((n, 128, KC, 128), f32)
        w = np.asarray(w, f32)
        for i, c0 in enumerate(col_starts):
            a[i] = w[:, c0:c0 + 128].reshape(KC, 128, 128).transpose(1, 0, 2)
        return a.reshape(n, 128, KC * 128)

    w_in = np.asarray(inp["w_in"], f32)[0]
    kcols = [WA + h * 128 for h in range(HA)] + [3 * WA + WB + h * 128 for h in range(HB)]
    qcols = [h * 128 for h in range(HA)] + [3 * WA + h * 128 for h in range(HB)]
    wqk = cols_layout(w_in, kcols + qcols)
    vcols = [2 * WA + h * 128 for h in range(HA)] + [3 * WA + 2 * WB + h * 128 for h in range(HB)]
    wv = cols_layout(w_in, vcols)
    wo = cols_layout(np.asarray(inp["w_out"], f32)[0], [m * 128 for m in range(KC)])

    def gl(v):
        return np.asarray(v, f32).reshape(KC, 128).T
    gains = np.concatenate([
        gl(inp["ffn1_norm"][0]), gl(inp["mix_norm"][0]),
        gl(np.concatenate([np.asarray(inp["out_norm_a"][0], f32), np.asarray(inp["out_norm_b"][0], f32)])),
        gl(inp["ffn2_norm"][0]), gl(inp["final_norm"])], 1)
    gains = np.ascontiguousarray(gains, f32)

    consts = np.zeros((128, 256), f32)
    consts[:, 0:128] = np.eye(128, dtype=f32)
    for d in range(16):
        consts[d + 16, 128 + d] = -1.0
        consts[d, 128 + d + 16] = 1.0

    p = np.arange(128)[:, None]
    f = np.arange(TT)[None, :]
    maskA = np.zeros((128, 20, TT), f32)
    for kb in range(20):
        dl = -1024 + kb * 128 + p - f
        ad = np.abs(dl)
        maskA[:, kb, :] = ((ad <= 64).astype(f32) + ((dl % 4 == 0) & (ad <= 256)).astype(f32)
                           + ((dl % 16 == 0) & (ad <= 1024)).astype(f32))
    maskA = maskA.reshape(128, 20 * TT)

    rel = np.asarray(inp["nbr_rel_bias"], f32)[0]
    gb = np.zeros((HB, 128, 8, TT), f32)
    for kb in range(8):
        jrel = -256 + kb * 128 + p
        drow = (jrel // 64) - (f // 64)
        dr = drow + 7
        dc = np.clip((jrel % 64) - (f % 64), -15, 15) + 15
        ok = (dr >= 0) & (dr <= 14)
        g = rel[:, np.clip(dr, 0, 14), dc]
        gb[:, :, kb, :] = np.where(ok[None], g, 0.0)
    gb = gb.reshape(HB, 128, 8 * TT)
    relb = np.ascontiguousarray(rel.reshape(HB, 465))

    inv = (ROPE_THETA ** (-np.arange(0, 32, 2, dtype=f32) / f32(32))).astype(f32)

    shared = dict(wg1=gu_layout(inp["ffn1_w_gate"][0]), wu1=gu_layout(inp["ffn1_w_up"][0]), wd1=d_layout(inp["ffn1_w_down"][0]),
                  wg2=gu_layout(inp["ffn2_w_gate"][0]), wu2=gu_layout(inp["ffn2_w_up"][0]), wd2=d_layout(inp["ffn2_w_down"][0]),
                  wqk=wqk, wv=wv, wo=wo, gains=gains, consts=consts, relb=relb, gb=gb, maskA=maskA)
    in_maps = []
    for c in range(cfg.NCORES):
        g0 = c * OWN - HALO
        gidx = np.arange(g0, g0 + EXT)
        valid = (gidx >= 0) & (gidx < NTOK)
        gcl = np.clip(gidx, 0, NTOK - 1)
        x_ext = np.where(valid[:, None], stream[gcl], f32(0.0)).astype(f32)
        e_seq = np.where(valid, seq_id[gcl], -1)
        e_pos = np.where(valid, pos[gcl], 0)
        e_len = np.where(valid, slen[gcl], 64 * 8)
        ang = (e_pos.astype(f32)[None, :] * inv[:, None]).astype(f32)
        ropeC = np.ones((128, EXT), f32); ropeS = np.zeros((128, EXT), f32)
        ropeC[0:16] = np.cos(ang); ropeC[16:32] = np.cos(ang)
        ropeS[0:16] = np.sin(ang); ropeS[16:32] = np.sin(ang)
        vflag = np.zeros((NTO, 20), f32)
        maskB = np.zeros((NTO, 128, 8, TT), f32)
        for ti in range(NTO):
            i0 = (ti + HT) * TT
            qs = e_seq[i0]
            for kb in range(20):
                j0 = i0 - 1024 + kb * 128
                vflag[ti, kb] = 1.0 if (e_seq[j0] == qs and qs >= 0) else 0.0
            qpos = e_pos[i0:i0 + TT]; rows = e_len[i0] // 64
            r = qpos // 64; cc = qpos % 64
            rs = np.clip(r - 4, 0, rows - 8); cs = np.clip(cc - 8, 0, 64 - 16)
            for kb in range(8):
                j0 = i0 - 256 + kb * 128
                kp = e_pos[j0:j0 + 128]; ks = e_seq[j0:j0 + 128]
                rho = kp // 64; gam = kp % 64
                ok = ((ks[:, None] == qs) & (qs >= 0) & (rho[:, None] >= rs[None, :]) & (rho[:, None] < rs[None, :] + 8)
                      & (gam[:, None] >= cs[None, :]) & (gam[:, None] < cs[None, :] + 16))
                maskB[ti, :, kb, :] = ok.astype(f32)
        m = dict(shared)
        m.update(x_ext=x_ext, ropeC=ropeC, ropeS=ropeS,
                 vflag=np.ascontiguousarray(np.broadcast_to(vflag.reshape(1, NTO * 20), (128, NTO * 20)), f32),
                 maskB=maskB.reshape(NTO, 128, 8 * TT))
        in_maps.append(m)
    return in_maps


def run(cfg, inp):
    in_maps = host_prepare(cfg, inp)
    nc = build_program(cfg)
    res = run_bass_kernel_spmd(nc, in_maps, core_ids=list(range(cfg.NCORES)))
    ys = np.concatenate([np.asarray(r["y"], np.float32) for r in res.results], 0)
    return ys


def kernel(x_prompt, x_sample, **w):
    cfg = Cfg()
    inp = dict(w)
    inp["x_prompt"] = x_prompt
    inp["x_sample"] = x_sample
    ys = run(cfg, inp)
    D = cfg.D
    npr = x_prompt.shape[0] * x_prompt.shape[1]
    y_prompt = ys[:npr].reshape(x_prompt.shape).astype(np.float32)
    y_sample = ys[npr:].reshape(x_sample.shape).astype(np.float32)
    return (y_prompt, y_sample)
```

```python
import numpy as np
import concourse.bass as bass
import concourse.mybir as mybir
from concourse.bass_utils import run_bass_kernel_spmd

F32 = mybir.dt.float32
BF16 = mybir.dt.bfloat16
AF = mybir.ActivationFunctionType
ALU = mybir.AluOpType
AX = mybir.AxisListType

HALO = 1024
TT = 512
ROPE_THETA = 500000.0
EPS = 1e-6


class Cfg:
    def __init__(self, D=4096, DFF=11008, HA=16, HB=16, NCORES=8, OWN=3072,
                 SEQS=(8192, 8192, 4096, 4096)):
        self.D, self.DFF, self.HA, self.HB = D, DFF, HA, HB
        self.NCORES, self.OWN, self.SEQS = NCORES, OWN, tuple(SEQS)
        self.KC = D // 128
        self.FC = DFF // 128
        self.FG = self.FC // 2
        self.NH = HA + HB
        self.EXT = OWN + 2 * HALO
        self.NTE = self.EXT // TT
        self.NTO = OWN // TT
        self.HT = HALO // TT
        assert self.FC % 2 == 0 and HA % 2 == 0 and HB % 2 == 0
        assert (HA + HB) * 128 == D and self.KC % 4 == 0
        assert sum(SEQS) == NCORES * OWN
        self.WSLOT = max(self.KC * 128, self.FG * 128)
        self.ABYTES = max(self.FG * TT * 2, 43008, 2 * D * 4)
        self.BIGW = self.WSLOT * 4 >= 28 * TT


class Op:
    __slots__ = ("eng", "emit", "deps", "chan", "idx")


class Sched:
    def __init__(self):
        self.ops = []
        self.lastw = {}
        self.readers = {}
        self.eng_count = {}
        self.chan_count = {}

    def add(self, eng, emit, reads=(), writes=(), chan=None):
        op = Op()
        op.eng, op.emit, op.chan = eng, emit, chan
        deps = set()
        for r in reads:
            w = self.lastw.get(r)
            if w is not None:
                deps.add(w)
        for w_ in writes:
            w = self.lastw.get(w_)
            if w is not None:
                deps.add(w)
            rs = self.readers.get(w_)
            if rs:
                deps.update(rs)
        op.deps = deps
        for r in reads:
            self.readers.setdefault(r, []).append(op)
        for w_ in writes:
            self.lastw[w_] = op
            self.readers[w_] = []
        if chan is None:
            self.eng_count[eng] = self.eng_count.get(eng, 0) + 1
            op.idx = self.eng_count[eng]
        else:
            self.chan_count[chan] = self.chan_count.get(chan, 0) + 16
            op.idx = self.chan_count[chan]
        self.ops.append(op)
        return op


def build_program(cfg):
    D, KC, FC, FG = cfg.D, cfg.KC, cfg.FC, cfg.FG
    HA, HB, NH = cfg.HA, cfg.HB, cfg.NH
    OWN, EXT, NTE, NTO, HT = cfg.OWN, cfg.EXT, cfg.NTE, cfg.NTO, cfg.HT
    NBLK = EXT // 128
    WSLOT = cfg.WSLOT
    NBW = 4
    SCALE = 128.0 ** -0.5

    nc = bass.Bass("TRN2", target_bir_lowering=False)

    def din(name, shape, dt=F32):
        return nc.dram_tensor(name, list(shape), dt, kind="ExternalInput").ap()

    x_ext = din("x_ext", [EXT, D])
    wg = [din("wg1", [FC, 128, KC * 128]), din("wg2", [FC, 128, KC * 128])]
    wu = [din("wu1", [FC, 128, KC * 128]), din("wu2", [FC, 128, KC * 128])]
    wd = [din("wd1", [2 * KC, 128, FG * 128]), din("wd2", [2 * KC, 128, FG * 128])]
    wqk = din("wqk", [2 * NH, 128, KC * 128])
    wv = din("wv", [NH, 128, KC * 128])
    wo = din("wo", [KC, 128, KC * 128])
    gains_d = din("gains", [128, 5 * KC])
    ropeC_d = din("ropeC", [128, EXT])
    ropeS_d = din("ropeS", [128, EXT])
    consts_d = din("consts", [128, 256])
    relb_d = din("relb", [HB, 465])
    gb_d = din("gb", [HB, 128, 8 * TT])
    maskA_d = din("maskA", [128, 20 * TT])
    vflag_d = din("vflag", [128, NTO * 20])
    maskB_d = din("maskB", [NTO, 128, 8 * TT])
    y_d = nc.dram_tensor("y", [OWN, D], F32, kind="ExternalOutput").ap()
    hscr = nc.dram_tensor("hscr", [NTO, 128, KC * TT], F32, kind="Internal").ap()
    qscr = nc.dram_tensor("qscr", [NH, 128, OWN], BF16, kind="Internal").ap()
    kscr = nc.dram_tensor("kscr", [NH, 128, EXT], BF16, kind="Internal").ap()
    vscr = nc.dram_tensor("vscr", [NH, 128, NBLK * 128], BF16, kind="Internal").ap()

    S = Sched()
    import contextlib
    es = contextlib.ExitStack()
    with es:
        def sb(name, shape, dt):
            return es.enter_context(nc.sbuf_tensor("s_" + name, list(shape), dt))

        Hs = sb("H", [128, KC, TT], F32)
        Xs = sb("X", [128, KC, TT], BF16)
        Ar = sb("A", [128, cfg.ABYTES // 2], BF16)
        Wall = sb("Wall", [128, NBW * WSLOT], BF16)
        Wr = [Wall[:, i * WSLOT:(i + 1) * WSLOT] for i in range(NBW)]
        ident = sb("ident", [128, 128], F32)
        piT = sb("piT", [128, 128], F32)
        identb = sb("identb", [128, 128], BF16)
        onesb = sb("onesb", [128, 128], BF16)
        gains = sb("gains", [128, 5 * KC], F32)
        RS = sb("RS", [128, TT], F32)
        SQ = [sb(f"SQ{i}", [128, TT], BF16) for i in range(2)]
        TS = [sb(f"TS{i}", [128, TT], BF16) for i in range(2)]
        SQ2 = [sb(f"SQ2{i}", [128, TT], BF16) for i in range(2)]
        CT = sb("CT", [128, TT], F32)
        ST = sb("ST", [128, TT], F32)
        QF = [sb(f"QF{i}", [128, TT], F32) for i in range(2)]
        T1 = sb("T1", [128, TT], F32)
        T2 = sb("T2", [128, TT], F32)
        RS2 = T2
        OUTB = TS
        VO = [sb(f"VO{i}", [128, TT], BF16) for i in range(2)]
        KMAX = sb("KMAX", [128, NH], F32)
        QMAX = sb("QMAX", [128, NH], F32)
        NEGM = sb("NEGM", [128, NH], F32)
        BSS = sb("BSS", [128, HB], F32)
        TM = [sb(f"TM{i}", [128, 1], F32) for i in range(2)]
        VFL = sb("VFL", [128, NTO * 20], F32)
        RD = T1
        EB2 = VO[0]
        PB2 = VO[1]
        PS = [es.enter_context(nc.psum_tensor(f"ps{i}", [128, TT], F32)) for i in range(8)]

        A_act = Ar[:, 0:FG * TT].rearrange("p (f t) -> p f t", t=TT)
        XS = [Ar[:, i * 2 * D:(i + 1) * 2 * D].bitcast(F32) for i in range(2)]
        o = 0
        QT = []; KT = []; VB = []; GB = []; EB = []; PB = []
        for i in range(2):
            QT.append(Ar[:, o:o + TT]); o += TT
        for i in range(2):
            KT.append(Ar[:, o:o + 2560]); o += 2560
        for i in range(2):
            VB.append(Ar[:, o:o + 2560].rearrange("p (b d) -> p b d", d=128)); o += 2560
        for i in range(2):
            GB.append(Ar[:, o:o + 8 * TT].rearrange("p (b t) -> p b t", t=TT)); o += 8 * TT
        for i in range(2):
            EB.append(Ar[:, o:o + TT]); o += TT
        for i in range(2):
            PB.append(Ar[:, o:o + TT]); o += TT
        assert o * 2 <= cfg.ABYTES
        A_RES = 1024

        def ares(lo_el, n_el):
            lo = lo_el * 2; hi = (lo_el + n_el) * 2
            return [("A", i) for i in range(lo // A_RES, (hi + A_RES - 1) // A_RES)]

        r_act = lambda f: ares(f * TT, TT)
        r_xs = lambda i: ares(i * 2 * D, 2 * D)
        _o = [0]

        def _nx(n):
            r = ares(_o[0], n); _o[0] += n; return r
        r_QT = [_nx(TT) for i in range(2)]
        r_KT = [_nx(2560) for i in range(2)]
        r_VB = [_nx(2560) for i in range(2)]
        r_GB = [_nx(8 * TT) for i in range(2)]
        r_EB = [_nx(TT) for i in range(2)]
        r_PB = [_nx(TT) for i in range(2)]
        if cfg.BIGW:
            MA = lambda kb: Wall[:, kb * TT:(kb + 1) * TT]
            MBv = lambda kb: Wall[:, (20 + kb) * TT:(21 + kb) * TT]
            ma_dst = [(Wall[:, 0:20 * TT], 0, 20)]
            mb_dst = Wall[:, 20 * TT:28 * TT]
            r_mask = [("W", i) for i in range((28 * TT + WSLOT - 1) // WSLOT)]
        else:
            MAt = sb("MAt", [128, 20 * TT], BF16)
            MBt = sb("MBt", [128, 8 * TT], BF16)
            MA = lambda kb: MAt[:, kb * TT:(kb + 1) * TT]
            MBv = lambda kb: MBt[:, kb * TT:(kb + 1) * TT]
            ma_dst = [(MAt[:, :], 0, 20)]
            mb_dst = MBt[:, :]
            r_mask = [("MASK",)]

        bank_rr = [0]

        def gbank():
            b = bank_rr[0] % 5
            bank_rr[0] += 1
            return b

        wslot_rr = [0]

        def wload(src_ap, nel):
            s = wslot_rr[0] % NBW
            wslot_rr[0] += 1
            S.add("pool", lambda e, s=s, src_ap=src_ap, nel=nel: e.dma_start(out=Wr[s][:, 0:nel], in_=src_ap),
                  writes=[("W", s)], chan=("W", s))
            return s

        pe_defer = []

        def defer(fn, delay=0):
            pe_defer.append([delay, fn])

        def flush_defer():
            cur = list(pe_defer)
            del pe_defer[:]
            for ent in cur:
                if ent[0] <= 0:
                    ent[1]()
                else:
                    ent[0] -= 1
                    pe_defer.append(ent)

        def drain_defer():
            while pe_defer:
                flush_defer()

        sq_pending = []

        def sq_accum(k, nchunks=None):
            n = KC if nchunks is None else nchunks
            if (k % 2) in sq_pending:
                flush_defer()
            assert (k % 2) not in sq_pending
            S.add("act", lambda e, k=k: e.activation(out=SQ[k % 2][:, :], in_=Hs[:, k, :], func=AF.Square),
                  reads=[r_H(k)], writes=[("SQ", k % 2)])
            sq_pending.append(k % 2)

            def pe_part(k=k):
                sq_pending.remove(k % 2)
                S.add("pe", lambda e: e.matmul(PS[6][:, :], lhsT=onesb[:, :], rhs=SQ[k % 2][:, :],
                                               start=(k == 0), stop=(k == n - 1)),
                      reads=[("SQ", k % 2), "onesb"], writes=[("PS", 6)])
            defer(pe_part)

        def mm_group(bank, nmm, lhs_fn, rhs_fn, reads, out_ap=None):
            outp = PS[bank][:, :] if out_ap is None else out_ap

            def emit(e):
                ins = None
                for k in range(nmm):
                    ins = e.matmul(outp, lhsT=lhs_fn(k), rhs=rhs_fn(k), start=(k == 0), stop=(k == nmm - 1))
                return ins
            S.add("pe", emit, reads=reads, writes=[("PS", bank)])

        S.add("sp", lambda e: e.dma_start(out=ident[:, :], in_=consts_d[:, 0:128]), writes=["ident"], chan="c_id")
        S.add("sp", lambda e: e.dma_start(out=piT[:, :], in_=consts_d[:, 128:256]), writes=["piT"], chan="c_pi")
        S.add("sp", lambda e: e.dma_start(out=gains[:, :], in_=gains_d[:, :]), writes=["gains"], chan="c_g")
        S.add("sp", lambda e: e.dma_start(out=VFL[:, :], in_=vflag_d[:, :]), writes=["VFL"], chan="c_vf")
        S.add("dve", lambda e: e.memset(onesb[:, :], 1.0), writes=["onesb"])
        S.add("dve", lambda e: e.tensor_copy(out=identb[:, :], in_=ident[:, :]), reads=["ident"], writes=["identb"])
        S.add("dve", lambda e: e.memset(KMAX[:, :], 0.0), writes=["KMAX"])
        S.add("dve", lambda e: e.memset(QMAX[:, :], 0.0), writes=["QMAX"])
        S.add("dve", lambda e: e.memset(BSS[:, :], 0.0), writes=["BSS"])

        r_H = lambda k: ("H", k)
        r_X = lambda k: ("X", k)

        def norm(gidx, final=False):
            drain_defer()
            S.add("dve", lambda e: e.tensor_scalar(out=RS[:, :], in0=PS[6][:, :], scalar1=1.0 / D, scalar2=EPS,
                                                   op0=ALU.mult, op1=ALU.add),
                  reads=[("PS", 6)], writes=["RS"])
            S.add("act", lambda e: e.activation(out=RS[:, :], in_=RS[:, :], func=AF.Sqrt), reads=["RS"], writes=["RS"])
            S.add("dve", lambda e: e.reciprocal(out=RS[:, :], in_=RS[:, :]), reads=["RS"], writes=["RS"])
            for k in range(KC):
                if final:
                    S.add("dve", lambda e, k=k: e.scalar_tensor_tensor(
                        out=Hs[:, k, :], in0=Hs[:, k, :], scalar=gains[:, gidx * KC + k:gidx * KC + k + 1],
                        in1=RS[:, :], op0=ALU.mult, op1=ALU.mult),
                        reads=[r_H(k), "RS", "gains"], writes=[r_H(k)])
                else:
                    S.add("dve", lambda e, k=k: e.scalar_tensor_tensor(
                        out=Xs[:, k, :], in0=Hs[:, k, :], scalar=gains[:, gidx * KC + k:gidx * KC + k + 1],
                        in1=RS[:, :], op0=ALU.mult, op1=ALU.mult),
                        reads=[r_H(k), "RS", "gains"], writes=[r_X(k)])

        def ffn(li):
            xr = [r_X(k) for k in range(KC)]
            for g in range(2):
                for fl in range(FG):
                    f = g * FG + fl
                    s = wload(wg[li][f], KC * 128)
                    wg_ = Wr[s][:, 0:KC * 128].rearrange("p (k c) -> p k c", k=KC)
                    s2 = wload(wu[li][f], KC * 128)
                    wu_ = Wr[s2][:, 0:KC * 128].rearrange("p (k c) -> p k c", k=KC)
                    bg = gbank(); bu = gbank()
                    mm_group(bg, KC, lambda k, wg_=wg_: wg_[:, k, :], lambda k: Xs[:, k, :], reads=[("W", s)] + xr)
                    mm_group(bu, KC, lambda k, wu_=wu_: wu_[:, k, :], lambda k: Xs[:, k, :], reads=[("W", s2)] + xr)
                    flush_defer()
                    S.add("act", lambda e, bg=bg, fl=fl: e.activation(out=TS[fl % 2][:, :], in_=PS[bg][:, :], func=AF.Silu),
                          reads=[("PS", bg)], writes=[("TS", fl % 2)])
                    S.add("dve", lambda e, bu=bu, fl=fl: e.tensor_tensor(out=A_act[:, fl, :], in0=TS[fl % 2][:, :],
                                                                       in1=PS[bu][:, :], op=ALU.mult),
                          reads=[("PS", bu), ("TS", fl % 2)], writes=r_act(fl))
                ar = [r for fl in range(FG) for r in r_act(fl)]
                for m in range(KC):
                    s = wload(wd[li][g * KC + m], FG * 128)
                    wv_ = Wr[s][:, 0:FG * 128].rearrange("p (k c) -> p k c", k=FG)
                    b = gbank()
                    mm_group(b, FG, lambda k, wv_=wv_: wv_[:, k, :], lambda k: A_act[:, k, :], reads=[("W", s)] + ar)
                    flush_defer()
                    S.add("dve", lambda e, b=b, m=m: e.scalar_tensor_tensor(
                        out=Hs[:, m, :], in0=PS[b][:, :], scalar=0.5, in1=Hs[:, m, :], op0=ALU.mult, op1=ALU.add),
                        reads=[("PS", b), r_H(m)], writes=[r_H(m)])
                    if g == 1:
                        sq_accum(m)

        def qk_chunk(t, idx, h, is_q, is_a, wview, s, j):
            xr = [r_X(k) for k in range(KC)]
            b = gbank()
            mm_group(b, KC, lambda k: wview[:, k, :], lambda k: Xs[:, k, :], reads=[("W", s)] + xr)
            flush_defer()
            ob = idx % 2
            sc = SCALE if is_q else 1.0
            if is_a:
                S.add("act", lambda e: e.activation(out=QF[ob][:, :], in_=PS[b][:, :], func=AF.Copy, scale=sc),
                      reads=[("PS", b)], writes=[("QF", ob)])

                def d1():
                    S.add("pe", lambda e: e.matmul(PS[7][:, :], lhsT=piT[:, :], rhs=QF[ob][:, :], start=True, stop=True),
                          reads=[("QF", ob), "piT"], writes=[("PS", 7)])
                    S.add("dve", lambda e: e.tensor_tensor(out=T1[:, :], in0=QF[ob][:, :], in1=CT[:, :], op=ALU.mult),
                          reads=[("QF", ob), "CT"], writes=["T1"])
                    S.add("dve", lambda e: e.tensor_tensor(out=T2[:, :], in0=PS[7][:, :], in1=ST[:, :], op=ALU.mult),
                          reads=[("PS", 7), "ST"], writes=["T2"])
                    S.add("dve", lambda e: e.tensor_tensor(out=OUTB[ob][:, :], in0=T1[:, :], in1=T2[:, :], op=ALU.add),
                          reads=["T1", "T2"], writes=[("TS", ob)])
                    tail()
                defer(d1)
            else:
                S.add("act", lambda e: e.activation(out=OUTB[ob][:, :], in_=PS[b][:, :], func=AF.Copy, scale=sc),
                      reads=[("PS", b)], writes=[("TS", ob)])
                defer(lambda: tail())

            def tail():
                S.add("act", lambda e: e.activation(out=SQ2[ob][:, :], in_=OUTB[ob][:, :], func=AF.Square),
                      reads=[("TS", ob)], writes=[("SQ2", ob)])
                defer(tail2)

            def tail2():
                S.add("pe", lambda e: e.matmul(PS[5][:, :], lhsT=onesb[:, :], rhs=SQ2[ob][:, :], start=True, stop=True),
                      reads=[("SQ2", ob), "onesb"], writes=[("PS", 5)])
                S.add("dve", lambda e: e.tensor_reduce(out=TM[ob][:, :], in_=PS[5][:, :], axis=AX.X, op=ALU.max),
                      reads=[("PS", 5)], writes=[("TM", ob)])
                MX = QMAX if is_q else KMAX
                mxn = "QMAX" if is_q else "KMAX"
                S.add("dve", lambda e: e.tensor_tensor(out=MX[:, h:h + 1], in0=MX[:, h:h + 1], in1=TM[ob][:, :], op=ALU.max),
                      reads=[("TM", ob), mxn], writes=[mxn])
                if is_q:
                    to = t - HT
                    dst = qscr[h, :, to * TT:(to + 1) * TT]
                else:
                    dst = kscr[h, :, t * TT:(t + 1) * TT]
                S.add("sp", lambda e: e.dma_start(out=dst, in_=OUTB[ob][:, :]), reads=[("TS", ob)],
                      writes=[("scr", "q" if is_q else "k", h, t)], chan=("TS", ob))

        def xload_tb(t, tb):
            sl = tb % 2
            r0 = t * TT + tb * 128
            S.add("sp", lambda e: e.dma_start(out=XS[sl], in_=x_ext[r0:r0 + 128, :]), writes=r_xs(sl), chan=("XS", sl))

        def xtrans_tb(t, tb):
            sl = tb % 2
            for kq in range(KC // 4):
                b = gbank()

                def emit(e, kq=kq, b=b):
                    ins = None
                    for j in range(4):
                        ins = e.transpose(PS[b][:, j * 128:(j + 1) * 128], XS[sl][:, (kq * 4 + j) * 128:(kq * 4 + j + 1) * 128], ident[:, :])
                    return ins
                S.add("pe", emit, reads=r_xs(sl) + ["ident"], writes=[("PS", b)])
                flush_defer()
                outv = Hs[:, kq * 4:(kq + 1) * 4, tb * 128:(tb + 1) * 128]
                inv = PS[b][:, :].rearrange("p (a c) -> p a c", a=4)
                if kq % 2 == 0:
                    S.add("dve", lambda e, outv=outv, inv=inv: e.tensor_copy(out=outv, in_=inv),
                          reads=[("PS", b)], writes=[r_H(kq * 4 + j) for j in range(4)])
                else:
                    S.add("act", lambda e, outv=outv, inv=inv: e.activation(out=outv, in_=inv, func=AF.Copy),
                          reads=[("PS", b)], writes=[r_H(kq * 4 + j) for j in range(4)])
                if tb == 3:
                    for j in range(4):
                        sq_accum(kq * 4 + j)

        def xprep_all(t):
            xload_tb(t, 0); xload_tb(t, 1)
            for tb in range(4):
                xtrans_tb(t, tb)
                if tb + 2 < 4:
                    xload_tb(t, tb + 2)

        xprep_all(0)
        for t in range(NTE):
            own = HT <= t < HT + NTO
            S.add("sp", lambda e, t=t: e.dma_start(out=CT[:, :], in_=ropeC_d[:, t * TT:(t + 1) * TT]), writes=["CT"], chan="ropeC")
            S.add("sp", lambda e, t=t: e.dma_start(out=ST[:, :], in_=ropeS_d[:, t * TT:(t + 1) * TT]), writes=["ST"], chan="ropeS")
            norm(0)
            ffn(0)
            if t + 1 < NTE:
                xload_tb(t + 1, 0); xload_tb(t + 1, 1)
            norm(1)
            if own:
                S.add("sp", lambda e, t=t: e.dma_start(out=hscr[t - HT], in_=Hs[:, :, :].rearrange("p k t -> p (k t)")),
                      reads=[r_H(k) for k in range(KC)], writes=[("hscr", t - HT)], chan="hst")
            idx = 0
            nxt_tb = [0]
            far = (t == 0) or (t == NTE - 1)
            for h in range(NH):
                if far and h >= HA:
                    continue
                s = wload(wqk[h], KC * 128)
                wview = Wr[s][:, 0:KC * 128].rearrange("p (k c) -> p k c", k=KC)
                qk_chunk(t, idx, h, False, h < HA, wview, s, 0); idx += 1
                if t + 1 < NTE and idx % 4 == 0 and nxt_tb[0] < 4:
                    xtrans_tb(t + 1, nxt_tb[0])
                    if nxt_tb[0] + 2 < 4:
                        xload_tb(t + 1, nxt_tb[0] + 2)
                    nxt_tb[0] += 1
            while t + 1 < NTE and nxt_tb[0] < 4:
                xtrans_tb(t + 1, nxt_tb[0])
                if nxt_tb[0] + 2 < 4:
                    xload_tb(t + 1, nxt_tb[0] + 2)
                nxt_tb[0] += 1
            xr = [r_X(k) for k in range(KC)]
            vcnt = 0
            for h in range(NH):
                if far and h >= HA:
                    continue
                s = wload(wv[h], KC * 128)
                wview = Wr[s][:, 0:KC * 128].rearrange("p (k c) -> p k c", k=KC)
                b = gbank()
                for tb in range(4):
                    mm_group(b, KC, lambda k, tb=tb: Xs[:, k, tb * 128:(tb + 1) * 128], lambda k, wview=wview: wview[:, k, :],
                             reads=[("W", s)] + xr, out_ap=PS[b][:, tb * 128:(tb + 1) * 128])
                flush_defer()
                vb = vcnt % 2; vcnt += 1
                S.add("act", lambda e, b=b, vb=vb: e.activation(out=VO[vb][:, :], in_=PS[b][:, :], func=AF.Copy),
                      reads=[("PS", b)], writes=[("VO", vb)])
                S.add("sp", lambda e, h=h, vb=vb, t=t: e.dma_start(out=vscr[h, :, t * TT:(t + 1) * TT], in_=VO[vb][:, :]),
                      reads=[("VO", vb)], writes=[("scr", "v", h, t)], chan=("VO", vb))
            if own:
                for h in range(NH):
                    s = wload(wqk[NH + h], KC * 128)
                    wview = Wr[s][:, 0:KC * 128].rearrange("p (k c) -> p k c", k=KC)
                    qk_chunk(t, idx, h, True, h < HA, wview, s, 0); idx += 1
            flush_defer()

        drain_defer()
        for hb in range(HB):
            S.add("sp", lambda e, hb=hb: e.dma_start(out=T1[:, 0:465], in_=relb_d[hb:hb + 1, :].to_broadcast([128, 465])),
                  writes=["T1"], chan="relb")
            S.add("act", lambda e, hb=hb: e.activation(out=T2[:, 0:465], in_=T1[:, 0:465], func=AF.Square,
                                                       accum_out=BSS[:, hb:hb + 1]),
                  reads=["T1", "BSS"], writes=["T2", "BSS"])
        S.add("dve", lambda e: e.tensor_tensor(out=NEGM[:, :], in0=QMAX[:, :], in1=KMAX[:, :], op=ALU.mult),
              reads=["QMAX", "KMAX"], writes=["NEGM"])
        S.add("act", lambda e: e.activation(out=NEGM[:, :], in_=NEGM[:, :], func=AF.Sqrt), reads=["NEGM"], writes=["NEGM"])
        S.add("act", lambda e: e.activation(out=BSS[:, :], in_=BSS[:, :], func=AF.Sqrt), reads=["BSS"], writes=["BSS"])
        S.add("dve", lambda e: e.tensor_scalar(out=NEGM[:, :], in0=NEGM[:, :], scalar1=-1.02, scalar2=None, op0=ALU.mult),
              reads=["NEGM"], writes=["NEGM"])
        S.add("dve", lambda e: e.tensor_tensor(out=NEGM[:, HA:NH], in0=NEGM[:, HA:NH], in1=BSS[:, :], op=ALU.subtract),
              reads=["NEGM", "BSS"], writes=["NEGM"])

        def attn_loads(ti, h):
            par = h % 2
            i0 = (ti + HT) * TT
            is_a = h < HA
            klo = i0 - 1024 if is_a else i0 - 256
            nk = 2560 if is_a else 1024
            S.add("sp", lambda e: e.dma_start(out=QT[par], in_=qscr[h, :, ti * TT:(ti + 1) * TT]),
                  reads=[("scr", "q", h, ti + HT)], writes=r_QT[par], chan=("QT", par))
            S.add("sp", lambda e: e.dma_start(out=KT[par][:, 0:nk], in_=kscr[h, :, klo:klo + nk]),
                  reads=[("scr", "k", h, tt) for tt in range(klo // TT, (klo + nk - 1) // TT + 1)], writes=r_KT[par], chan=("KT", par))
            b0 = klo // 128
            nb = nk // 128
            S.add("sp", lambda e: e.dma_start(out=VB[par][:, 0:nb, :].rearrange("p b d -> p (b d)"), in_=vscr[h, :, b0 * 128:(b0 + nb) * 128]),
                  reads=[("scr", "v", h, tt) for tt in range(klo // TT, (klo + nk - 1) // TT + 1)], writes=r_VB[par], chan=("VB", par))
            if not is_a:
                S.add("pool", lambda e: e.dma_start(out=GB[par].rearrange("p b t -> p (b t)"), in_=gb_d[h - HA]),
                      writes=r_GB[par], chan=("GB", par))

        for ti in range(NTO):
            for (dst, k0, n) in ma_dst:
                S.add("pool", lambda e, dst=dst, k0=k0, n=n: e.dma_start(out=dst, in_=maskA_d[:, k0 * TT:(k0 + n) * TT]),
                      writes=r_mask, chan="MA")
            S.add("pool", lambda e, ti=ti: e.dma_start(out=mb_dst, in_=maskB_d[ti]), writes=r_mask, chan="MA")
            attn_loads(ti, 0)
            EBL = [EB[0], EB[1], EB2[:, :]]
            PBL = [PB[0], PB[1], PB2[:, :]]
            r_EBL = [r_EB[0], r_EB[1], [("VO", 0)]]
            r_PBL = [r_PB[0], r_PB[1], [("VO", 1)]]
            DEPTH = 3

            def rs_group(rs_t, rn, wdt):
                S.add("dve", lambda e: e.tensor_scalar(out=rs_t[:, :], in0=PS[7][:, :], scalar1=1.0 / wdt, scalar2=EPS,
                                                       op0=ALU.mult, op1=ALU.add), reads=[("PS", 7)], writes=[rn])
                S.add("act", lambda e: e.activation(out=rs_t[:, :], in_=rs_t[:, :], func=AF.Sqrt), reads=[rn], writes=[rn])
                S.add("dve", lambda e: e.reciprocal(out=rs_t[:, :], in_=rs_t[:, :]), reads=[rn], writes=[rn])

            for h in range(NH):
                if h + 1 < NH:
                    attn_loads(ti, h + 1)
                par = h % 2
                is_a = h < HA
                nkb = 20 if is_a else 8
                bO = 3 + par; bD = 5 + par

                def s_op(kb, par=par, is_a=is_a, h=h):
                    bS = kb % DEPTH
                    if is_a:
                        S.add("pe", lambda e: e.matmul(PS[bS][:, :], lhsT=KT[par][:, kb * 128:(kb + 1) * 128], rhs=QT[par], start=True, stop=True),
                              reads=r_QT[par] + r_KT[par], writes=[("PS", bS)])
                    else:
                        def emit(e):
                            e.matmul(PS[bS][:, :], lhsT=KT[par][:, kb * 128:(kb + 1) * 128], rhs=QT[par], start=True, stop=False)
                            return e.matmul(PS[bS][:, :], lhsT=identb[:, :], rhs=GB[par][:, kb, :], start=False, stop=True)
                        S.add("pe", emit, reads=r_QT[par] + r_KT[par] + r_GB[par] + ["identb"], writes=[("PS", bS)])
                for kb in range(DEPTH):
                    s_op(kb)
                flush_defer()
                for kb in range(nkb):
                    bS = kb % DEPTH
                    eb = kb % DEPTH
                    S.add("act", lambda e, eb=eb, bS=bS, h=h: e.activation(out=EBL[eb], in_=PS[bS][:, :], func=AF.Exp,
                                                                         bias=NEGM[:, h:h + 1], scale=1.0),
                          reads=[("PS", bS), "NEGM"], writes=r_EBL[eb])
                    if is_a:
                        S.add("dve", lambda e, kb=kb, eb=eb, ti=ti: e.scalar_tensor_tensor(
                            out=PBL[eb], in0=EBL[eb], scalar=VFL[:, ti * 20 + kb:ti * 20 + kb + 1], in1=MA(kb),
                            op0=ALU.mult, op1=ALU.mult),
                            reads=r_EBL[eb] + r_mask + ["VFL"], writes=r_PBL[eb])
                    else:
                        S.add("dve", lambda e, kb=kb, eb=eb: e.tensor_tensor(out=PBL[eb], in0=EBL[eb], in1=MBv(kb), op=ALU.mult),
                              reads=r_EBL[eb] + r_mask, writes=r_PBL[eb])

                    def emit(e, kb=kb, eb=eb, par=par, bO=bO, bD=bD, nkb=nkb):
                        e.matmul(PS[bO][:, :], lhsT=VB[par][:, kb, :], rhs=PBL[eb], start=(kb == 0), stop=(kb == nkb - 1))
                        return e.matmul(PS[bD][:, :], lhsT=onesb[:, :], rhs=PBL[eb], start=(kb == 0), stop=(kb == nkb - 1))
                    S.add("pe", emit, reads=r_PBL[eb] + r_VB[par] + ["onesb"], writes=[("PS", bO), ("PS", bD)])
                    if kb + DEPTH < nkb:
                        s_op(kb + DEPTH)
                    if kb == 2 or kb == 5:
                        flush_defer()
                        if h == HA and kb == 5:
                            rs_group(RS, "RS", HA * 128)
                first = (h == 0) or (h == HA)
                last = (h == HA - 1) or (h == NH - 1)

                def fin(h=h, bO=bO, bD=bD, first=first, last=last):
                    S.add("dve", lambda e: e.reciprocal(out=RD[:, :], in_=PS[bD][:, :]), reads=[("PS", bD)], writes=["T1"])
                    S.add("dve", lambda e: e.tensor_tensor(out=Hs[:, h, :], in0=PS[bO][:, :], in1=RD[:, :], op=ALU.mult),
                          reads=[("PS", bO), "T1"], writes=[r_H(h)])
                    S.add("act", lambda e: e.activation(out=SQ[h % 2][:, :], in_=Hs[:, h, :], func=AF.Square),
                          reads=[r_H(h)], writes=[("SQ", h % 2)])

                    def dss():
                        S.add("pe", lambda e: e.matmul(PS[7][:, :], lhsT=onesb[:, :], rhs=SQ[h % 2][:, :], start=first, stop=last),
                              reads=[("SQ", h % 2), "onesb"], writes=[("PS", 7)])
                    defer(dss)
                defer(fin, delay=1)
            drain_defer()
            rs_group(RS2, "T2", HB * 128)
            for k in range(KC):
                rs_t = RS if k < HA else RS2
                rn = "RS" if k < HA else "T2"
                S.add("dve", lambda e, k=k, rs_t=rs_t: e.scalar_tensor_tensor(
                    out=Xs[:, k, :], in0=Hs[:, k, :], scalar=gains[:, 2 * KC + k:2 * KC + k + 1], in1=rs_t[:, :],
                    op0=ALU.mult, op1=ALU.mult),
                    reads=[r_H(k), rn, "gains"], writes=[r_X(k)])
            S.add("sp", lambda e, ti=ti: e.dma_start(out=Hs[:, :, :].rearrange("p k t -> p (k t)"), in_=hscr[ti]),
                  reads=[("hscr", ti)], writes=[r_H(k) for k in range(KC)], chan="hld")
            xr = [r_X(k) for k in range(KC)]
            for m in range(KC):
                s = wload(wo[m], KC * 128)
                wview = Wr[s][:, 0:KC * 128].rearrange("p (k c) -> p k c", k=KC)
                b = gbank()
                mm_group(b, KC, lambda k, wview=wview: wview[:, k, :], lambda k: Xs[:, k, :], reads=[("W", s)] + xr)
                flush_defer()
                S.add("dve", lambda e, b=b, m=m: e.tensor_tensor(out=Hs[:, m, :], in0=PS[b][:, :], in1=Hs[:, m, :], op=ALU.add),
                      reads=[("PS", b), r_H(m)], writes=[r_H(m)])
                sq_accum(m)
            norm(3)
            ffn(1)
            norm(4, final=True)
            for tb in range(4):
                sl = tb % 2
                for kq in range(KC // 4):
                    b = gbank()

                    def emit(e, tb=tb, kq=kq, b=b):
                        ins = None
                        for j in range(4):
                            ins = e.transpose(PS[b][:, j * 128:(j + 1) * 128], Hs[:, kq * 4 + j, tb * 128:(tb + 1) * 128], ident[:, :])
                        return ins
                    S.add("pe", emit, reads=[r_H(kq * 4 + j) for j in range(4)] + ["ident"], writes=[("PS", b)])
                    rr = ares(sl * 2 * D + kq * 1024, 1024)
                    if kq % 2 == 0:
                        S.add("dve", lambda e, sl=sl, kq=kq, b=b: e.tensor_copy(out=XS[sl][:, kq * 512:(kq + 1) * 512], in_=PS[b][:, :]),
                              reads=[("PS", b)], writes=rr)
                    else:
                        S.add("act", lambda e, sl=sl, kq=kq, b=b: e.activation(out=XS[sl][:, kq * 512:(kq + 1) * 512], in_=PS[b][:, :], func=AF.Copy),
                              reads=[("PS", b)], writes=rr)
                r0 = ti * TT + tb * 128
                S.add("sp", lambda e, sl=sl, r0=r0: e.dma_start(out=y_d[r0:r0 + 128, :], in_=XS[sl]),
                      reads=r_xs(sl), writes=[("y", r0)], chan=("XS", sl))

        engs = ["pe", "act", "dve", "pool", "sp"]
        chans = sorted(S.chan_count.keys(), key=str)
        sem_e = {e: es.enter_context(nc.semaphore(f"e_{e}")) for e in ["pe", "act", "dve", "pool"] if S.eng_count.get(e)}
        sem_c = {c: es.enter_context(nc.semaphore(f"c{i}")) for i, c in enumerate(chans)}

        def run_engine(eng_name, e):
            waited = {}
            for op in S.ops:
                if op.eng != eng_name:
                    continue
                need = {}
                for d in op.deps:
                    if d.chan is None:
                        if d.eng == eng_name and eng_name == "pe":
                            continue
                        key = ("e", d.eng)
                        sem = sem_e[d.eng]
                    else:
                        key = ("c", d.chan)
                        sem = sem_c[d.chan]
                    if need.get(key, (None, 0))[1] < d.idx:
                        need[key] = (sem, d.idx)
                for key, (sem, val) in need.items():
                    if waited.get(key, 0) >= val:
                        continue
                    e.wait_ge(sem, val)
                    waited[key] = val
                ins = op.emit(e)
                if op.chan is None:
                    ins.then_inc(sem_e[op.eng], 1)
                else:
                    ins.then_inc(sem_c[op.chan], 16)
            if eng_name == "sp":
                for c in chans:
                    e.wait_ge(sem_c[c], S.chan_count[c])

        with nc.Block() as block:
            @block.tensor
            def _(e):
                run_engine("pe", e)

            @block.scalar
            def _(e):
                run_engine("act", e)

            @block.vector
            def _(e):
                run_engine("dve", e)

            @block.gpsimd
            def _(e):
                run_engine("pool", e)

            @block.sync
            def _(e):
                run_engine("sp", e)
    return nc


def _structure(cfg):
    seq_id = np.concatenate([np.full(L, i, np.int64) for i, L in enumerate(cfg.SEQS)])
    pos = np.concatenate([np.arange(L, dtype=np.int64) for L in cfg.SEQS])
    slen = np.concatenate([np.full(L, L, np.int64) for L in cfg.SEQS])
    return seq_id, pos, slen


def host_prepare(cfg, inp):
    D, KC, FC, FG, HA, HB, NH = cfg.D, cfg.KC, cfg.FC, cfg.FG, cfg.HA, cfg.HB, cfg.NH
    OWN, EXT, NTO, HT = cfg.OWN, cfg.EXT, cfg.NTO, cfg.HT
    WA, WB = HA * 128, HB * 128
    f32 = np.float32
    stream = np.concatenate([np.asarray(inp["x_prompt"], f32).reshape(-1, D), np.asarray(inp["x_sample"], f32).reshape(-1, D)], 0)
    NTOK = stream.shape[0]
    seq_id, pos, slen = _structure(cfg)

    def gu_layout(wg):
        a = np.ascontiguousarray(np.asarray(wg, f32).reshape(KC, 128, FC, 128).transpose(2, 1, 0, 3))
        return a.reshape(FC, 128, KC * 128)

    def d_layout(wdn):
        a = np.asarray(wdn, f32).reshape(2, FG, 128, KC, 128).transpose(0, 3, 2, 1, 4)
        return np.ascontiguousarray(a).reshape(2 * KC, 128, FG * 128)

    def cols_layout(w, col_starts):
        n = len(col_starts)
        a = np.empty((n, 128, KC, 128), f32)
        w = np.asarray(w, f32)
        for i, c0 in enumerate(col_starts):
            a[i] = w[:, c0:c0 + 128].reshape(KC, 128, 128).transpose(1, 0, 2)
        return a.reshape(n, 128, KC * 128)

    w_in = np.asarray(inp["w_in"], f32)[0]
    kcols = [WA + h * 128 for h in range(HA)] + [3 * WA + WB + h * 128 for h in range(HB)]
    qcols = [h * 128 for h in range(HA)] + [3 * WA + h * 128 for h in range(HB)]
    wqk = cols_layout(w_in, kcols + qcols)
    vcols = [2 * WA + h * 128 for h in range(HA)] + [3 * WA + 2 * WB + h * 128 for h in range(HB)]
    wv = cols_layout(w_in, vcols)
    wo = cols_layout(np.asarray(inp["w_out"], f32)[0], [m * 128 for m in range(KC)])

    def gl(v):
        return np.asarray(v, f32).reshape(KC, 128).T
    gains = np.concatenate([
        gl(inp["ffn1_norm"][0]), gl(inp["mix_norm"][0]),
        gl(np.concatenate([np.asarray(inp["out_norm_a"][0], f32), np.asarray(inp["out_norm_b"][0], f32)])),
        gl(inp["ffn2_norm"][0]), gl(inp["final_norm"])], 1)
    gains = np.ascontiguousarray(gains, f32)

    consts = np.zeros((128, 256), f32)
    consts[:, 0:128] = np.eye(128, dtype=f32)
    for d in range(16):
        consts[d + 16, 128 + d] = -1.0
        consts[d, 128 + d + 16] = 1.0

    p = np.arange(128)[:, None]
    f = np.arange(TT)[None, :]
    maskA = np.zeros((128, 20, TT), f32)
    for kb in range(20):
        dl = -1024 + kb * 128 + p - f
        ad = np.abs(dl)
        maskA[:, kb, :] = ((ad <= 64).astype(f32) + ((dl % 4 == 0) & (ad <= 256)).astype(f32)
                           + ((dl % 16 == 0) & (ad <= 1024)).astype(f32))
    maskA = maskA.reshape(128, 20 * TT)

    rel = np.asarray(inp["nbr_rel_bias"], f32)[0]
    gb = np.zeros((HB, 128, 8, TT), f32)
    for kb in range(8):
        jrel = -256 + kb * 128 + p
        drow = (jrel // 64) - (f // 64)
        dr = drow + 7
        dc = np.clip((jrel % 64) - (f % 64), -15, 15) + 15
        ok = (dr >= 0) & (dr <= 14)
        g = rel[:, np.clip(dr, 0, 14), dc]
        gb[:, :, kb, :] = np.where(ok[None], g, 0.0)
    gb = gb.reshape(HB, 128, 8 * TT)
    relb = np.ascontiguousarray(rel.reshape(HB, 465))

    inv = (ROPE_THETA ** (-np.arange(0, 32, 2, dtype=f32) / f32(32))).astype(f32)

    shared = dict(wg1=gu_layout(inp["ffn1_w_gate"][0]), wu1=gu_layout(inp["ffn1_w_up"][0]), wd1=d_layout(inp["ffn1_w_down"][0]),
                  wg2=gu_layout(inp["ffn2_w_gate"][0]), wu2=gu_layout(inp["ffn2_w_up"][0]), wd2=d_layout(inp["ffn2_w_down"][0]),
                  wqk=wqk, wv=wv, wo=wo, gains=gains, consts=consts, relb=relb, gb=gb, maskA=maskA)
    in_maps = []
    for c in range(cfg.NCORES):
        g0 = c * OWN - HALO
        gidx = np.arange(g0, g0 + EXT)
        valid = (gidx >= 0) & (gidx < NTOK)
        gcl = np.clip(gidx, 0, NTOK - 1)
        x_ext = np.where(valid[:, None], stream[gcl], f32(0.0)).astype(f32)
        e_seq = np.where(valid, seq_id[gcl], -1)
        e_pos = np.where(valid, pos[gcl], 0)
        e_len = np.where(valid, slen[gcl], 64 * 8)
        ang = (e_pos.astype(f32)[None, :] * inv[:, None]).astype(f32)
        ropeC = np.ones((128, EXT), f32); ropeS = np.zeros((128, EXT), f32)
        ropeC[0:16] = np.cos(ang); ropeC[16:32] = np.cos(ang)
        ropeS[0:16] = np.sin(ang); ropeS[16:32] = np.sin(ang)
        vflag = np.zeros((NTO, 20), f32)
        maskB = np.zeros((NTO, 128, 8, TT), f32)
        for ti in range(NTO):
            i0 = (ti + HT) * TT
            qs = e_seq[i0]
            for kb in range(20):
                j0 = i0 - 1024 + kb * 128
                vflag[ti, kb] = 1.0 if (e_seq[j0] == qs and qs >= 0) else 0.0
            qpos = e_pos[i0:i0 + TT]; rows = e_len[i0] // 64
            r = qpos // 64; cc = qpos % 64
            rs = np.clip(r - 4, 0, rows - 8); cs = np.clip(cc - 8, 0, 64 - 16)
            for kb in range(8):
                j0 = i0 - 256 + kb * 128
                kp = e_pos[j0:j0 + 128]; ks = e_seq[j0:j0 + 128]
                rho = kp // 64; gam = kp % 64
                ok = ((ks[:, None] == qs) & (qs >= 0) & (rho[:, None] >= rs[None, :]) & (rho[:, None] < rs[None, :] + 8)
                      & (gam[:, None] >= cs[None, :]) & (gam[:, None] < cs[None, :] + 16))
                maskB[ti, :, kb, :] = ok.astype(f32)
        m = dict(shared)
        m.update(x_ext=x_ext, ropeC=ropeC, ropeS=ropeS,
                 vflag=np.ascontiguousarray(np.broadcast_to(vflag.reshape(1, NTO * 20), (128, NTO * 20)), f32),
                 maskB=maskB.reshape(NTO, 128, 8 * TT))
        in_maps.append(m)
    return in_maps


def run(cfg, inp):
    in_maps = host_prepare(cfg, inp)
    nc = build_program(cfg)
    res = run_bass_kernel_spmd(nc, in_maps, core_ids=list(range(cfg.NCORES)))
    ys = np.concatenate([np.asarray(r["y"], np.float32) for r in res.results], 0)
    return ys


def kernel(x_prompt, x_sample, **w):
    cfg = Cfg()
    inp = dict(w)
    inp["x_prompt"] = x_prompt
    inp["x_sample"] = x_sample
    ys = run(cfg, inp)
    D = cfg.D
    npr = x_prompt.shape[0] * x_prompt.shape[1]
    y_prompt = ys[:npr].reshape(x_prompt.shape).astype(np.float32)
    y_sample = ys[npr:].reshape(x_sample.shape).astype(np.float32)
    return (y_prompt, y_sample)
```
